# Optimizing a Trainium2 kernel written in Bass

```python
import jax, jax.numpy as jnp
from jax import lax
import numpy as np

D_MODEL = 1024
BATCH = 4
SEQ = 4096
DEPTH = 4
DEC_BATCH = 16
DEC_SEQ = 16
PAST_LEN = 4096

CHUNK = 64
LEFT_CHUNKS = 8
BAND_LEFT = LEFT_CHUNKS * CHUNK
BAND = BAND_LEFT + CHUNK
N_HEADS = 16
HEAD_DIM = D_MODEL // N_HEADS
CONV_WIDTH = 31
CONV_HIST = CONV_WIDTH - 1
FFN_HIDDEN = -(-8 * D_MODEL // (3 * 256)) * 256
REL_CLIP = 128
N_REL = 2 * REL_CLIP + 1
N_A_LAYERS = DEPTH // 2
N_B_LAYERS = DEPTH - N_A_LAYERS
EPS = 1e-6
NEG_INF = -1e30

kernel_name = "yoco_conformer_chunk_band_attention_stream_step"


def rms_norm(x, g):
    xf = x.astype(jnp.float32)
    y = xf * lax.rsqrt(jnp.mean(xf * xf, axis=-1, keepdims=True) + EPS)
    return (y * g.astype(jnp.float32)).astype(x.dtype)


def layer_norm(x, g, b):
    xf = x.astype(jnp.float32)
    mu = jnp.mean(xf, axis=-1, keepdims=True)
    var = jnp.mean(jnp.square(xf - mu), axis=-1, keepdims=True)
    y = (xf - mu) * lax.rsqrt(var + EPS)
    return (y * g.astype(jnp.float32) + b.astype(jnp.float32)).astype(x.dtype)


def swiglu_ffn(h, g, w_in, w_out):
    a, b = jnp.split(rms_norm(h, g) @ w_in, 2, axis=-1)
    return h + (jax.nn.silu(a) * b) @ w_out


def conv_module(h, hist, g, w_pw1, b_pw1, w_dw, b_dw, ln_g, ln_b, w_pw2, b_pw2):
    a, gate = jnp.split(rms_norm(h, g) @ w_pw1 + b_pw1, 2, axis=-1)
    glu = a * jax.nn.sigmoid(gate)
    full = jnp.concatenate([hist.astype(glu.dtype), glu], axis=1)
    y = lax.conv_general_dilated(
        full, w_dw[:, None, :].astype(full.dtype), window_strides=(1,), padding="VALID",
        dimension_numbers=("NWC", "WIO", "NWC"), feature_group_count=D_MODEL) + b_dw
    y = jax.nn.silu(layer_norm(y, ln_g, ln_b))
    return h + y @ w_pw2 + b_pw2, full[:, -CONV_HIST:]


def rel_bias_band(table, n_q, n_k, left):
    dist = jnp.arange(n_q)[:, None] + left - jnp.arange(n_k)[None, :]
    idx = jnp.clip(dist, -REL_CLIP, REL_CLIP) + REL_CLIP
    return table[:, idx].astype(jnp.float32)


def band_attention_prompt(q, k, v, table):
    B, T, H, Dh = q.shape
    nc = T // CHUNK
    scale = HEAD_DIM ** -0.5
    pad = ((0, 0), (BAND_LEFT, 0), (0, 0), (0, 0))
    kp = jnp.pad(k, pad)
    vp = jnp.pad(v, pad)
    bias = rel_bias_band(table, CHUNK, BAND, BAND_LEFT)
    q_chunks = q.reshape(B, nc, CHUNK, H, Dh).transpose(1, 0, 2, 3, 4)

    def one_chunk(args):
        c, q_blk = args
        start = c * CHUNK
        k_blk = lax.dynamic_slice_in_dim(kp, start, BAND, axis=1)
        v_blk = lax.dynamic_slice_in_dim(vp, start, BAND, axis=1)
        s = jnp.einsum("bqhd,bkhd->bhqk", q_blk, k_blk).astype(jnp.float32) * scale + bias
        valid = (start - BAND_LEFT + jnp.arange(BAND)) >= 0
        s = jnp.where(valid[None, None, None, :], s, NEG_INF)
        p = jax.nn.softmax(s, axis=-1).astype(v_blk.dtype)
        return jnp.einsum("bhqk,bkhd->bqhd", p, v_blk)

    out = lax.map(one_chunk, (jnp.arange(nc), q_chunks))
    return out.transpose(1, 0, 2, 3, 4).reshape(B, T, H * Dh)


def band_attention_sample(q, k_all, v_all, table):
    B, S, H, Dh = q.shape
    W = k_all.shape[1] - S
    scale = HEAD_DIM ** -0.5
    bias = rel_bias_band(table, S, W + S, W)
    s = jnp.einsum("bqhd,bkhd->bhqk", q, k_all).astype(jnp.float32) * scale + bias
    p = jax.nn.softmax(s, axis=-1).astype(v_all.dtype)
    return jnp.einsum("bhqk,bkhd->bqhd", p, v_all).reshape(B, S, H * Dh)


def run_trunk(x, conv_hist, past_k, past_v,
              norm_conv, w_pw1, b_pw1, w_dw, b_dw, ln_g, ln_b, w_pw2, b_pw2,
              norm_kv, w_kv, norm_attn, w_q, w_o, rel_bias,
              norm_ffn, w_ffn_in, w_ffn_out, norm_final):
    B, T, _ = x.shape
    h = x
    new_hist = []
    k = v = None
    for l in range(DEPTH):
        if l < N_A_LAYERS:
            h, st = conv_module(h, conv_hist[l], norm_conv[l], w_pw1[l], b_pw1[l], w_dw[l],
                                b_dw[l], ln_g[l], ln_b[l], w_pw2[l], b_pw2[l])
            new_hist.append(st)
        else:
            if l == N_A_LAYERS:
                k, v = jnp.split(rms_norm(h, norm_kv) @ w_kv, 2, axis=-1)
                k = k.reshape(B, T, N_HEADS, HEAD_DIM)
                v = v.reshape(B, T, N_HEADS, HEAD_DIM)
            j = l - N_A_LAYERS
            q = (rms_norm(h, norm_attn[j]) @ w_q[j]).reshape(B, T, N_HEADS, HEAD_DIM)
            if past_k is None:
                o = band_attention_prompt(q, k, v, rel_bias[j])
            else:
                o = band_attention_sample(q, jnp.concatenate([past_k.astype(k.dtype), k], axis=1),
                                          jnp.concatenate([past_v.astype(v.dtype), v], axis=1),
                                          rel_bias[j])
            h = h + o @ w_o[j]
        h = swiglu_ffn(h, norm_ffn[l], w_ffn_in[l], w_ffn_out[l])
    return rms_norm(h, norm_final), jnp.stack(new_hist, axis=0), k, v


def setup_inputs(seed: int = 0) -> dict:
    key = jax.random.key(seed)
    ks = jax.random.split(key, 32)
    D, F, HD = D_MODEL, FFN_HIDDEN, N_HEADS * HEAD_DIM
    cache_rows = min(BAND_LEFT, PAST_LEN)

    def nrm(k, shape, scale):
        return jax.random.normal(k, shape, jnp.float32) * scale

    return {
        "x_prompt": nrm(ks[0], (BATCH, SEQ, D), 1.0),
        "x_sample": nrm(ks[1], (DEC_BATCH, DEC_SEQ, D), 1.0),
        "cache_conv": nrm(ks[2], (N_A_LAYERS, DEC_BATCH, CONV_HIST, D), 0.5),
        "cache_k": nrm(ks[3], (DEC_BATCH, cache_rows, N_HEADS, HEAD_DIM), 1.0),
        "cache_v": nrm(ks[4], (DEC_BATCH, cache_rows, N_HEADS, HEAD_DIM), 1.0),
        "norm_conv": 1.0 + nrm(ks[5], (N_A_LAYERS, D), 0.01),
        "w_pw1": nrm(ks[6], (N_A_LAYERS, D, 2 * D), D ** -0.5),
        "b_pw1": nrm(ks[7], (N_A_LAYERS, 2 * D), 0.01),
        "w_dw": nrm(ks[8], (N_A_LAYERS, CONV_WIDTH, D), CONV_WIDTH ** -0.5),
        "b_dw": nrm(ks[9], (N_A_LAYERS, D), 0.01),
        "ln_g": 1.0 + nrm(ks[10], (N_A_LAYERS, D), 0.01),
        "ln_b": nrm(ks[11], (N_A_LAYERS, D), 0.01),
        "w_pw2": nrm(ks[12], (N_A_LAYERS, D, D), D ** -0.5),
        "b_pw2": nrm(ks[13], (N_A_LAYERS, D), 0.01),
        "norm_kv": 1.0 + nrm(ks[14], (D,), 0.01),
        "w_kv": nrm(ks[15], (D, 2 * HD), D ** -0.5),
        "norm_attn": 1.0 + nrm(ks[16], (N_B_LAYERS, D), 0.01),
        "w_q": nrm(ks[17], (N_B_LAYERS, D, HD), D ** -0.5),
        "w_o": nrm(ks[18], (N_B_LAYERS, HD, D), HD ** -0.5),
        "rel_bias": nrm(ks[19], (N_B_LAYERS, N_HEADS, N_REL), 0.2),
        "norm_ffn": 1.0 + nrm(ks[20], (DEPTH, D), 0.01),
        "w_ffn_in": nrm(ks[21], (DEPTH, D, 2 * F), D ** -0.5),
        "w_ffn_out": nrm(ks[22], (DEPTH, F, D), F ** -0.5),
        "norm_final": 1.0 + nrm(ks[23], (D,), 0.01),
    }


def reference(x_prompt, x_sample, cache_conv, cache_k, cache_v,
              norm_conv, w_pw1, b_pw1, w_dw, b_dw, ln_g, ln_b, w_pw2, b_pw2,
              norm_kv, w_kv, norm_attn, w_q, w_o, rel_bias,
              norm_ffn, w_ffn_in, w_ffn_out, norm_final):
    weights = (norm_conv, w_pw1, b_pw1, w_dw, b_dw, ln_g, ln_b, w_pw2, b_pw2,
               norm_kv, w_kv, norm_attn, w_q, w_o, rel_bias,
               norm_ffn, w_ffn_in, w_ffn_out, norm_final)
    B, T, _ = x_prompt.shape
    zero_hist = jnp.zeros((N_A_LAYERS, B, CONV_HIST, D_MODEL), x_prompt.dtype)
    y_prompt, conv_prompt, k_full, v_full = run_trunk(x_prompt, zero_hist, None, None, *weights)
    keep = min(BAND_LEFT, T)
    k_prompt = k_full[:, T - keep:]
    v_prompt = v_full[:, T - keep:]
    y_sample, conv_sample, k_sample, v_sample = run_trunk(x_sample, cache_conv, cache_k, cache_v, *weights)
    return (y_prompt, y_sample, conv_prompt, conv_sample, k_prompt, v_prompt, k_sample, v_sample)
```

```python
import numpy as np
import concourse.bass as bass
import concourse.mybir as mybir
from concourse.bass_utils import run_bass_kernel_spmd

F32 = mybir.dt.float32
F32R = mybir.dt.float32r
BF16 = mybir.dt.bfloat16
AF = mybir.ActivationFunctionType
ALU = mybir.AluOpType
AX = mybir.AxisListType

D = 1024
KC = 8
FF = 2816
FC = 22
NH = 16
HD = 64
CW = 31
CH = 30
NEG = -1.0e30
EPS = 1e-6
NCORE = 8
NTOK = 2048
HALO = 640
NX = NTOK + HALO
RING = 13
ENGS = ['pe', 'act', 'dve', 'pool', 'sp']
BLK = {'pe': 'tensor', 'act': 'scalar', 'dve': 'vector', 'pool': 'gpsimd', 'sp': 'sync'}


class Sched:
    def __init__(s):
        s.ops = {e: [] for e in ENGS}
        s.lastw = {}
        s.rds = {}
        s.chan_cnt = {}

    def op(s, eng, fn, r=(), w=(), chan=None, multi=False):
        idx = len(s.ops[eng])
        deps = {}

        def add(ref):
            if ref is None:
                return
            k, v = ref
            if deps.get(k, -1) < v:
                deps[k] = v
        r = [k_ for k_ in r if k_ != ('scr',)]
        w = [k_ for k_ in w if k_ != ('scr',)]
        for key in r:
            add(s.lastw.get(key))
        for key in w:
            add(s.lastw.get(key))
            for ref in s.rds.get(key, ()):
                add(ref)
        if chan is not None:
            c = s.chan_cnt.get(chan, 0) + 1
            s.chan_cnt[chan] = c
            ref = (('c', chan), c)
            if multi:
                deps.pop(('c', chan), None)
        else:
            ref = (('e', eng), idx)
        if eng == 'pe':
            deps.pop(('e', 'pe'), None)
        o = dict(fn=fn, deps=deps, sig=False, chan=chan, val=None)
        s.ops[eng].append(o)
        for key in w:
            s.lastw[key] = ref
            s.rds[key] = []
        for key in r:
            s.rds.setdefault(key, []).append(ref)
        return o

    def finalize(s):
        for e in ENGS:
            for o in s.ops[e]:
                for (k, name), v in o['deps'].items():
                    if k == 'e':
                        s.ops[name][v]['sig'] = True
        for e in ENGS:
            c = 0
            for o in s.ops[e]:
                if o['chan'] is None and o['sig']:
                    c += 1
                    o['val'] = c

    def emit(s, nc, block, sems, csems):
        for e in ENGS:
            if not s.ops[e]:
                continue

            def body(h, e=e):
                waited = {}
                for o in s.ops[e]:
                    for (k, name), v in o['deps'].items():
                        if k == 'e':
                            sem = sems[name]
                            val = s.ops[name][v]['val']
                        else:
                            sem = csems[name]
                            val = 16 * (s.chan_cnt[name] if name in ('init', 'cv') else v)
                        if waited.get((k, name), 0) >= val:
                            continue
                        waited[(k, name)] = val
                        h.wait_ge(sem, val)
                    ins = o['fn'](h)
                    if o['chan'] is not None:
                        ins.then_inc(csems[o['chan']], 16)
                    elif o['sig']:
                        ins.then_inc(sems[e], 1)
                if e == 'sp':
                    for name, cnt in s.chan_cnt.items():
                        h.wait_ge(csems[name], 16 * cnt)
            getattr(block, BLK[e])(body)


def vec_cols():
    cols = {}
    n = 0

    def add(name, k):
        nonlocal n
        cols[name] = n
        n += k
    for l in range(2):
        add(('norm_conv', l), 8)
        add(('b_pw1a', l), 8)
        add(('b_pw1g', l), 8)
        add(('b_dw', l), 8)
        add(('ln_g', l), 8)
        add(('ln_b', l), 8)
        add(('b_pw2', l), 8)
        add(('norm_attn', l), 8)
        add(('cB', l), 16)
    for l in range(4):
        add(('norm_ffn', l), 8)
    add('norm_kv', 8)
    add('norm_final', 8)
    return cols, n


VCOL, NV = vec_cols()


def build_program(group_sel=None):
    nc = bass.Bass("TRN2", target_bir_lowering=False)
    S = Sched()

    def din(name, shape):
        return nc.dram_tensor(name, shape, F32, kind="ExternalInput")

    def dout(name, shape):
        return nc.dram_tensor(name, shape, F32, kind="ExternalOutput")

    x_d = din("x", [NX, D]).ap()
    xs_d = din("xs", [32, D]).ap()
    cconv_d = din("cconv", [2, 2, CH, D]).ap()
    ck_d = din("ck", [2, 512, D]).ap()
    cv_d = din("cv", [2, 512, D]).ap()
    w_pw1_d = din("w_pw1", [2, D, 2 * D]).ap()
    w_pw2_d = din("w_pw2", [2, D, D]).ap()
    w_kv_d = din("w_kv", [D, 2 * D]).ap()
    w_q_d = din("w_q", [2, D, D]).ap()
    w_o_d = din("w_o", [2, D, D]).ap()
    w_in_d = din("w_ffn_in", [4, D, 2 * FF]).ap()
    w_out_d = din("w_ffn_out", [4, FF, D]).ap()
    vecs_d = din("vecs", [128, NV]).ap()
    wdw_d = din("wdw", [128, 2 * KC * CW]).ap()
    btab_d = din("btab", [2, NH, 128, 256]).ap()
    flag_d = din("flag", [128, 1]).ap()
    ident_d = din("ident", [128, 128]).ap()

    y_d = dout("y", [NTOK, D]).ap()
    ys_d = dout("ys", [32, D]).ap()
    convp_d = dout("convp", [2, CH, D]).ap()
    convs_d = dout("convs", [2, 2, CH, D]).ap()
    klast_d = dout("klast", [2, 512, 512]).ap()
    vlast_d = dout("vlast", [2, 512, 512]).ap()
    ksam_d = dout("ksam", [2, 32, 512]).ap()
    vsam_d = dout("vsam", [2, 32, 512]).ap()

    from contextlib import ExitStack
    es = ExitStack()

    def sb(name, shape, dt):
        return es.enter_context(nc.sbuf_tensor("sb_" + name, shape, dt))

    GM = 1024
    h = sb("h", [128, KC, GM], F32)
    xn = sb("xn", [128, KC, GM], BF16)
    NSCR = 8 * (GM + CH) + 8 * 512
    scr = sb("scr", [128, NSCR], F32)
    ktr = sb("ktr", [128, KC, RING * 128], BF16)
    vr = sb("vr", [128, RING, D], BF16)
    NSLOT = 3
    wsl = [sb(f"wsl{i}", [128, 4096], BF16) for i in range(NSLOT)]
    vecs = sb("vecs", [128, NV], F32)
    wdw = sb("wdw", [128, 2 * KC * CW], F32)
    flag = sb("flag", [128, 1], F32)
    hist = sb("hist", [128, 2, KC, CH], F32)
    ones_r = sb("ones_r", [128, 128], F32R)
    ones_f = sb("ones_f", [128, 128], F32)
    ident_f = sb("ident_f", [128, 128], F32)
    ident_b = sb("ident_b", [128, 128], BF16)
    ones_b = sb("ones_b", [1, 128], BF16)
    maskrow = sb("maskrow", [1, 128], BF16)
    epst = sb("epst", [128, 1], F32)
    cA = sb("cA", [128, 2, NH], F32)
    rowmask = sb("rowmask", [128, 1], F32)
    sq = [sb(f"sq{i}", [128, 512], F32R) for i in range(4)]
    t32 = [sb(f"t32_{i}", [128, 512], F32) for i in range(4)]
    stg = [sb(f"stg{i}", [128, D], F32) for i in range(2)]
    small = sb("small", [128, 64], F32)
    h_s = sb("h_s", [128, KC, 32], F32)
    NDG = 8
    dg = [sb(f"dg{i}", [128, 128], BF16) for i in range(NDG)]
    assert 11072 + 1024 <= NSCR
    ps = [es.enter_context(nc.psum_tensor(f"ps{i}", [128, 512], F32)) for i in range(8)]

    full = scr[:, 0:8 * (GM + CH)].rearrange("p (c n) -> p c n", c=8)
    FBW = GM + 32
    fullb = scr[:, 0:4 * FBW].bitcast(BF16).rearrange("p (c n) -> p c n", c=8)
    fulls = scr[:, 4 * FBW:4 * FBW + 8 * 92].rearrange("p (c n) -> p c n", c=8)
    yc = scr[:, 8 * (GM + CH):8 * (GM + CH) + 4096].rearrange("p (c n) -> p c n", c=8)
    gbuf = scr[:, 0:11264].bitcast(BF16).rearrange("p (f n) -> p f n", f=FC)
    QT = scr[:, 0:4096].bitcast(BF16).rearrange("p (c n) -> p c n", c=8)
    biasT = scr[:, 4096:8192].rearrange("p (h n) -> p h n", h=NH)
    Sb = [scr[:, 8192 + i * 640:8192 + (i + 1) * 640] for i in range(2)]
    Pb = [scr[:, 9472 + i * 320:9472 + (i + 1) * 320].bitcast(BF16) for i in range(2)]
    PT = [scr[:, 10112 + i * 320:10112 + (i + 1) * 320].bitcast(BF16).rearrange("p (b n) -> p b n", b=5) for i in range(3)]
    Otok = [scr[:, 11072 + i * 512:11072 + (i + 1) * 512].bitcast(BF16) for i in range(2)]
    ynf = scr[:, 0:8192].rearrange("p (c n) -> p c n", c=8)

    scr_state = {'stage': None, 'fence': []}

    def scr_keys(stage):
        return ('scrstage',)

    V = lambda name, c=None: vecs[:, VCOL[name] + (0 if c is None else c):VCOL[name] + (0 if c is None else c) + 1]

    def dma(q, out, in_, r, w, chan, multi=False, **kw):
        S.op(q, lambda hh: hh.dma_start(out=out, in_=in_, **kw), r, w, chan=chan, multi=multi)

    def MM(out, lhsT, rhs, start, stop, r, w):
        S.op('pe', lambda hh: hh.matmul(out, lhsT, rhs, start=start, stop=stop), r, w)

    def TR(out, in_, ident, r, w):
        S.op('pe', lambda hh: hh.transpose(out, in_, ident), r, w)

    def ACT(out, in_, func, r, w, bias=None, scale=None, accum_out=None, tag=None):
        kw = {}
        if bias is not None:
            kw['bias'] = bias
        if scale is not None:
            kw['scale'] = scale
        if accum_out is not None:
            kw['accum_out'] = accum_out
        S.op('act', lambda hh: hh.activation(out, in_, func, **kw), r, w)

    def TT(out, in0, in1, op, r, w, eng='dve'):
        S.op(eng, lambda hh: hh.tensor_tensor(out, in0, in1, op), r, w)

    def TS(out, in0, s1, s2, op0, op1, r, w, accum_out=None, eng='dve'):
        if op1 is None:
            S.op(eng, lambda hh: hh.tensor_scalar(out, in0, s1, s2, op0, accum_out=accum_out), r, w)
        else:
            S.op(eng, lambda hh: hh.tensor_scalar(out, in0, s1, s2, op0, op1, accum_out=accum_out), r, w)

    def STT(out, in0, sc, in1, op0, op1, r, w):
        S.op('dve', lambda hh: hh.scalar_tensor_tensor(out, in0, sc, in1, op0, op1), r, w)

    def CP(out, in_, r, w, eng='dve'):
        S.op(eng, lambda hh: hh.tensor_copy(out, in_), r, w)

    def MEMSET(ap, val, w, eng='dve'):
        S.op(eng, lambda hh: hh.memset(ap, val), (), w)

    def RECIP(out, in_, r, w):
        S.op('dve', lambda hh: hh.reciprocal(out, in_), r, w)

    SCR = ('scr',)
    rr = {'mm': 0, 'w': 0, 'sq': 0, 't32': 0, 'stg': 0, 'otok': 0, 'dg': 0}

    def nxt(name, n):
        v = rr[name]
        rr[name] = (v + 1) % n
        return v

    def load_cols(wd, col_lists):
        si = nxt('w', NSLOT)
        sv = wsl[si][:, :].rearrange("p (c n) -> p c n", c=8)
        src = wd.rearrange("(c p) n -> p c n", p=128)
        o = 0
        for (c0, ncol) in col_lists:
            dma('pool', sv[:, :, o:o + ncol], src[:, :, c0:c0 + ncol], (), [('wsl', si)], chan=f"wsl{si}", multi=(o > 0))
            o += ncol
        return si, sv

    def load_wout(wd, j):
        si = nxt('w', NSLOT)
        sv = wsl[si][:, 0:FC * 128].rearrange("p (f n) -> p f n", f=FC)
        src = wd.rearrange("(f p) n -> p f n", p=128)
        dma('pool', sv, src[:, :, j * 128:(j + 1) * 128], (), [('wsl', si)], chan=f"wsl{si}")
        return si, sv

    dma('sp', vecs[:, :], vecs_d, (), ['vecs'], chan='init')
    dma('sp', wdw[:, :], wdw_d, (), ['wdw'], chan='init')
    dma('sp', flag[:, :], flag_d, (), ['flag'], chan='init')
    MEMSET(ones_f[:, :], 1.0 / D, ['ones_f'])
    ACT(ones_r[:, :], ones_f[:, :], AF.Copy, ['ones_f'], ['ones_r'])
    MEMSET(ones_b[:, :], 1.0, ['ones_b'])
    MEMSET(epst[:, :], EPS, ['epst'])
    MEMSET(hist[:, :, :, :], 0.0, ['hist'])
    dma('sp', ident_f[:, :], ident_d, (), ['ident_f'], chan='init')
    CP(ident_b[:, :], ident_f[:, :], ['ident_f'], ['ident_b'])
    MEMSET(rowmask[:, :], 0.0, ['rowmask'])
    MEMSET(rowmask[64:128, :], NEG, ['rowmask'])
    TS(maskrow[0:1, :], ones_b[0:1, :], flag[0:1, 0:1], -1.0, ALU.mult, ALU.add, ['ones_b', 'flag'], ['maskrow'])
    TS(maskrow[0:1, :], maskrow[0:1, :], -NEG, None, ALU.mult, None, ['maskrow'], ['maskrow'])
    for l in range(2):
        TS(cA[:, l, :], vecs[:, VCOL[('cB', l)]:VCOL[('cB', l)] + NH], rowmask[:, 0:1], None, ALU.add, None,
           ['vecs', 'rowmask'], [('cA', l)])

    def load_x(src_ap, tiles, src_base=0, ti0=0):
        for ti_, (off, w) in enumerate(tiles):
            ti = ti0 + ti_
            nb = max(1, w // 128)
            for b in range(nb):
                bw = min(128, w)
                si = nxt('stg', 2)
                dma('sp', stg[si][0:bw, :], src_ap[off - src_base + b * 128:off - src_base + b * 128 + bw, :], (), [('stg', si)], chan=f"stg{si}")
                for half in range(2):
                    bk = nxt('mm', 4)
                    for cc in range(4):
                        c = half * 4 + cc
                        TR(ps[bk][:, cc * 128:cc * 128 + bw], stg[si][0:bw, c * 128:(c + 1) * 128], ident_f[0:bw, 0:bw],
                           [('stg', si), 'ident_f'], [('ps', bk)])
                    o3 = h[:, half * 4:half * 4 + 4, off + b * 128:off + b * 128 + bw]
                    i3 = ps[bk][:, :].rearrange("p (c n) -> p c n", c=4)[:, :, 0:bw]
                    wk = [('h', half * 4 + cc, ti) for cc in range(4)]
                    if half == 0:
                        ACT(o3, i3, AF.Copy, [('ps', bk)], wk)
                    else:
                        CP(o3, i3, [('ps', bk)], wk)

    def rmsnorm(tiles, gname, dst, dkey):
        for ti, (off, w) in enumerate(tiles):
            for c in range(KC):
                qi = nxt('sq', 4)
                ACT(sq[qi][:, 0:w], h[:, c, off:off + w], AF.Square, [('h', c, ti)], [('sq', qi)])
                MM(ps[7][:, 0:w], ones_r[:, :], sq[qi][:, 0:w], c == 0, c == KC - 1, [('sq', qi), 'ones_r'], [('ps', 7)])
            ti2 = nxt('t32', 4)
            ACT(t32[ti2][:, 0:w], ps[7][:, 0:w], AF.Sqrt, [('ps', 7), 'epst'], [('t32', ti2)], bias=epst[:, 0:1])
            RECIP(t32[ti2][:, 0:w], t32[ti2][:, 0:w], [('t32', ti2)], [('t32', ti2)])
            for c in range(KC):
                STT(dst[:, c, off:off + w], h[:, c, off:off + w], V(gname, c), t32[ti2][:, 0:w], ALU.mult, ALU.mult,
                    [('h', c, ti), ('t32', ti2), 'vecs'], [(dkey, c, ti), SCR] if dkey == 'ynf' else [(dkey, c, ti)])

    def conv_layer(l, grp):
        tiles = grp['tiles']
        ntok = grp['ntok']
        stiles = grp.get('stiles', ())
        ptiles = [ti_ for ti_ in range(len(tiles)) if ti_ not in stiles]
        last_p = ptiles[-1] if ptiles else None
        SB0 = 704
        rmsnorm(tiles, ('norm_conv', l), xn, 'xn')
        if ptiles:
            CP(fullb[:, :, 0:CH], hist[:, l, :, :], ['hist'] + [('histw', j) for j in range(KC)], [('fullh',)])
        if stiles:
            for s_ in range(2):
                si = nxt('stg', 2)
                dma('sp', stg[si][0:CH, :], cconv_d[l, s_, :, :], (), [('stg', si)], chan=f"stg{si}")
                for half in range(2):
                    bk = nxt('mm', 4)
                    for cc in range(4):
                        c = half * 4 + cc
                        TR(ps[bk][:, cc * 128:cc * 128 + CH], stg[si][0:CH, c * 128:(c + 1) * 128], ident_f[0:CH, 0:CH],
                           [('stg', si), 'ident_f'], [('ps', bk)])
                    o3 = fulls[:, half * 4:half * 4 + 4, 46 * s_:46 * s_ + CH]
                    i3 = ps[bk][:, :].rearrange("p (c n) -> p c n", c=4)[:, :, 0:CH]
                    CP(o3, i3, [('ps', bk), SCR], [('fullsh',), SCR])
        for p in range(4):
            si, sv = load_cols(w_pw1_d[l], [(p * 256, 256), (D + p * 256, 256)])
            for ti, (off, w) in enumerate(tiles):
                for jj in range(2):
                    j = 2 * p + jj
                    ba = nxt('mm', 4)
                    bg = nxt('mm', 4)
                    for c in range(KC):
                        MM(ps[ba][:, 0:w], sv[:, c, jj * 128:(jj + 1) * 128], xn[:, c, off:off + w], c == 0, c == KC - 1,
                           [('wsl', si), ('xn', c, ti)], [('ps', ba)])
                    for c in range(KC):
                        MM(ps[bg][:, 0:w], sv[:, c, 256 + jj * 128:256 + (jj + 1) * 128], xn[:, c, off:off + w], c == 0, c == KC - 1,
                           [('wsl', si), ('xn', c, ti)], [('ps', bg)])
                    t = nxt('t32', 4)
                    ACT(t32[t][:, 0:w], ps[bg][:, 0:w], AF.Sigmoid, [('ps', bg), 'vecs'], [('t32', t)], bias=V(('b_pw1g', l), j))
                    if ti not in stiles:
                        dstv = fullb[:, j, CH + off:CH + off + w]
                        STT(dstv, ps[ba][:, 0:w], V(('b_pw1a', l), j), t32[t][:, 0:w], ALU.add, ALU.mult,
                            [('ps', ba), ('t32', t), 'vecs', SCR], [('full', j, ti), SCR])
                        if ti == last_p:
                            STT(hist[:, l, j, :], ps[ba][:, w - CH:w], V(('b_pw1a', l), j), t32[t][:, w - CH:w], ALU.add, ALU.mult,
                                [('ps', ba), ('t32', t), 'vecs', ('fullh',)], [('histw', j)])
                        if grp['kind'] == 'halo':
                            TS(dstv, dstv, flag[:, 0:1], None, ALU.mult, None, [('full', j, ti), 'flag'], [('full', j, ti)])
                            if ti == last_p:
                                TS(hist[:, l, j, :], hist[:, l, j, :], flag[:, 0:1], None, ALU.mult, None, [('histw', j), 'flag'], [('histw', j)])
                    else:
                        dstv = fulls[:, j, 0:92].rearrange("p (s t) -> p s t", s=2)[:, :, CH:CH + 16]
                        STT(dstv, ps[ba][:, 0:32].rearrange("p (s t) -> p s t", s=2), V(('b_pw1a', l), j),
                            t32[t][:, 0:32].rearrange("p (s t) -> p s t", s=2), ALU.add, ALU.mult,
                            [('ps', ba), ('t32', t), 'vecs', SCR], [('full', j, ti), SCR])
        allfull = [('full', j, ti_) for j in range(KC) for ti_ in stiles] + [('fullsh',)]
        if stiles:
            CP(fullb[:, :, SB0:SB0 + 92], fulls[:, :, :], allfull, [('fullsb',)])
            for s_ in range(2):
                si = nxt('stg', 2)
                for half in range(2):
                    bk = nxt('mm', 4)
                    for cc in range(4):
                        c = half * 4 + cc
                        TR(ps[bk][0:CH, cc * 128:(cc + 1) * 128], fulls[:, c, 46 * s_ + 16:46 * s_ + 46], ident_f[:, :],
                           allfull + ['ident_f'], [('ps', bk)])
                    CP(stg[si][0:CH, half * 512:(half + 1) * 512], ps[bk][0:CH, :], [('ps', bk)], [('stg', si)])
                dma('sp', convs_d[l, s_, :, :], stg[si][0:CH, :], [('stg', si)], [], chan=f"stg{si}")
        wv = wdw[:, :].rearrange("p (l c k) -> p l c k", l=2, c=KC)
        for ti, (off, w) in enumerate(tiles):
            rk = [('fullh',)] + [('full', None, t2) for t2 in range(ti + 1)]
            pending = []
            NPE = 24
            samp = ti in stiles
            for c in range(KC):
                bkD = 4 + (c % 2)
                bkP = 2 + (c % 2)
                if samp:
                    rkeys = [('fullsb',)]
                else:
                    rkeys = [('fullh',), ('full', c, ti)] + ([('full', c, ti - 1)] if (ti > 0 and (ti - 1) not in stiles) else [])

                def srcv(k, fv):
                    if not samp:
                        return fv[:, c, off + k:off + k + w]
                    return fv[:, c, SB0:SB0 + 92].rearrange("p (s t) -> p s t", s=2)[:, :, k:k + 16]

                def accv(bk):
                    if not samp:
                        return ps[bk][:, 0:w]
                    return ps[bk][:, 0:32].rearrange("p (s t) -> p s t", s=2)
                dstv = yc[:, c, 0:w] if not samp else yc[:, c, 0:32].rearrange("p (s t) -> p s t", s=2)
                for k in range(NPE):
                    di = nxt('dg', NDG)
                    if k % 2 == 0:
                        S.op('pool', lambda hh, o_=dg[di][:, :], s1=wv[:, l, c, k:k + 1]:
                             hh.tensor_scalar(o_, ident_f[:, :], s1, 0.0, ALU.mult, ALU.add), ['ident_f', 'wdw'], [('dg', di)])
                    else:
                        ACT(dg[di][:, :], ident_f[:, :], AF.Copy, ['ident_f', 'wdw'], [('dg', di)], scale=wv[:, l, c, k:k + 1])
                    MM(accv(bkP), dg[di][:, :], srcv(k, fullb), k == 0, k == NPE - 1, rkeys + [('dg', di)], [('ps', bkP)])
                for k in range(NPE, CW):
                    if k == NPE:
                        TS(accv(bkD), srcv(k, fullb), wv[:, l, c, k:k + 1], V(('b_dw', l), c), ALU.mult, ALU.add,
                           rkeys + ['wdw', 'vecs'], [('ps', bkD)])
                    elif k < CW - 1:
                        STT(accv(bkD), srcv(k, fullb), wv[:, l, c, k:k + 1], accv(bkD), ALU.mult, ALU.add,
                            rkeys + ['wdw', ('ps', bkD)], [('ps', bkD)])
                    else:
                        STT(dstv, srcv(k, fullb), wv[:, l, c, k:k + 1], accv(bkD), ALU.mult, ALU.add,
                            rkeys + ['wdw', ('ps', bkD)], [('yc', c)])
                if NPE == CW:
                    TS(dstv, accv(bkP), V(('b_dw', l), c), None, ALU.add, None, [('ps', bkP), 'vecs'], [('yc', c)])
                elif NPE > 0:
                    TT(dstv, dstv, accv(bkP), ALU.add, [('yc', c), ('ps', bkP)], [('yc', c)])
                def stats(c):
                    q1 = nxt('sq', 4)
                    ACT(sq[q1][:, 0:w], yc[:, c, 0:w], AF.Copy, [('yc', c)], [('sq', q1)])
                    MM(ps[6][:, 0:w], ones_r[:, :], sq[q1][:, 0:w], c == 0, c == KC - 1, [('sq', q1), 'ones_r'], [('ps', 6)])
                    q2 = nxt('sq', 4)
                    ACT(sq[q2][:, 0:w], yc[:, c, 0:w], AF.Square, [('yc', c)], [('sq', q2)])
                    MM(ps[7][:, 0:w], ones_r[:, :], sq[q2][:, 0:w], c == 0, c == KC - 1, [('sq', q2), 'ones_r'], [('ps', 7)])
                pending.append(c)
                if len(pending) > 1:
                    stats(pending.pop(0))
            while pending:
                stats(pending.pop(0))
            ta = nxt('t32', 4)
            tb = nxt('t32', 4)
            ACT(t32[ta][:, 0:w], ps[6][:, 0:w], AF.Copy, [('ps', 6)], [('t32', ta)])
            TT(t32[tb][:, 0:w], t32[ta][:, 0:w], ps[6][:, 0:w], ALU.mult, [('t32', ta), ('ps', 6)], [('t32', tb)])
            TT(t32[tb][:, 0:w], ps[7][:, 0:w], t32[tb][:, 0:w], ALU.subtract, [('t32', tb), ('ps', 7)], [('t32', tb)])
            TS(t32[tb][:, 0:w], t32[tb][:, 0:w], 0.0, None, ALU.max, None, [('t32', tb)], [('t32', tb)])
            ACT(t32[tb][:, 0:w], t32[tb][:, 0:w], AF.Sqrt, [('t32', tb), 'epst'], [('t32', tb)], bias=epst[:, 0:1])
            RECIP(t32[tb][:, 0:w], t32[tb][:, 0:w], [('t32', tb)], [('t32', tb)])
            for c in range(KC):
                TT(yc[:, c, 0:w], yc[:, c, 0:w], ps[6][:, 0:w], ALU.subtract, [('yc', c), ('ps', 6)], [('yc', c)])
                TT(yc[:, c, 0:w], yc[:, c, 0:w], t32[tb][:, 0:w], ALU.mult, [('yc', c), ('t32', tb)], [('yc', c)])
                ACT(xn[:, c, off:off + w], yc[:, c, 0:w], AF.Silu, [('yc', c), 'vecs'], [('xn', c, ti)],
                    bias=V(('ln_b', l), c), scale=V(('ln_g', l), c))
        for half in range(2):
            si, sv = load_cols(w_pw2_d[l], [(half * 512, 512)])
            for ti, (off, w) in enumerate(tiles):
                for jj in range(4):
                    j = half * 4 + jj
                    bk = nxt('mm', 4)
                    for c in range(KC):
                        MM(ps[bk][:, 0:w], sv[:, c, jj * 128:(jj + 1) * 128], xn[:, c, off:off + w], c == 0, c == KC - 1,
                           [('wsl', si), ('xn', c, ti)], [('ps', bk)])
                    STT(h[:, j, off:off + w], ps[bk][:, 0:w], V(('b_pw2', l), j), h[:, j, off:off + w], ALU.add, ALU.add,
                        [('ps', bk), ('h', j, ti), 'vecs'], [('h', j, ti)])

    def ffn_layer(l, grp):
        tiles = grp['tiles']
        rmsnorm(tiles, ('norm_ffn', l), xn, 'xn')
        for p in range(FC // 2):
            si, sv = load_cols(w_in_d[l], [(p * 256, 256), (FF + p * 256, 256)])
            for ti, (off, w) in enumerate(tiles):
                for jj in range(2):
                    f = 2 * p + jj
                    ba = nxt('mm', 4)
                    bb = nxt('mm', 4)
                    for c in range(KC):
                        MM(ps[ba][:, 0:w], sv[:, c, jj * 128:(jj + 1) * 128], xn[:, c, off:off + w], c == 0, c == KC - 1,
                           [('wsl', si), ('xn', c, ti)], [('ps', ba)])
                    for c in range(KC):
                        MM(ps[bb][:, 0:w], sv[:, c, 256 + jj * 128:256 + (jj + 1) * 128], xn[:, c, off:off + w], c == 0, c == KC - 1,
                           [('wsl', si), ('xn', c, ti)], [('ps', bb)])
                    t = nxt('t32', 4)
                    ACT(t32[t][:, 0:w], ps[ba][:, 0:w], AF.Silu, [('ps', ba)], [('t32', t)])
                    TT(gbuf[:, f, off:off + w], t32[t][:, 0:w], ps[bb][:, 0:w], ALU.mult, [('t32', t), ('ps', bb), SCR],
                       [('g', f, ti), SCR])
        for j in range(KC):
            si, sv = load_wout(w_out_d[l], j)
            for ti, (off, w) in enumerate(tiles):
                bk = nxt('mm', 4)
                for f in range(FC):
                    MM(ps[bk][:, 0:w], sv[:, f, :], gbuf[:, f, off:off + w], f == 0, f == FC - 1,
                       [('wsl', si), ('g', f, ti)], [('ps', bk)])
                TT(h[:, j, off:off + w], ps[bk][:, 0:w], h[:, j, off:off + w], ALU.add, [('ps', bk), ('h', j, ti)], [('h', j, ti)])

    def ring_runs(blk_a, nblk):
        runs = []
        b = 0
        while b < nblk:
            slot = (blk_a + b) % RING
            n = min(nblk - b, RING - slot)
            runs.append((b, slot, n))
            b += n
        return runs

    def kv_stage(grp):
        tiles = grp['tiles']
        stiles = grp.get('stiles', ())
        ptiles = [ti_ for ti_ in range(len(tiles)) if ti_ not in stiles]
        rmsnorm(tiles, 'norm_kv', xn, 'xn')
        for half in range(2):
            si, sv = load_cols(w_kv_d, [(half * 512, 512)])
            for ti, (off, w) in enumerate(tiles):
                if ti in stiles:
                    continue
                for jj in range(4):
                    j = half * 4 + jj
                    bk = nxt('mm', 4)
                    for c in range(KC):
                        MM(ps[bk][:, 0:w], sv[:, c, jj * 128:(jj + 1) * 128], xn[:, c, off:off + w], c == 0, c == KC - 1,
                           [('wsl', si), ('xn', c, ti)], [('ps', bk)])
                    for (lb, slot, n) in ring_runs(grp['blk0'] + off // 128, w // 128):
                        ACT(ktr[:, j, slot * 128:(slot + n) * 128], ps[bk][:, lb * 128:(lb + n) * 128], AF.Copy,
                            [('ps', bk)], [('ktr', j, s2) for s2 in range(slot, slot + n)])
            if ptiles:
                outblks = [b for b in range(grp['ntok'] // 128) if grp['kind'] == 'main' and grp['last'] and b >= grp['ntok'] // 128 - 4]
                for b in outblks:
                    bk = nxt('mm', 4)
                    tix = [i for i, (o_, w_) in enumerate(tiles) if o_ <= b * 128 < o_ + w_][0]
                    for c in range(KC):
                        MM(ps[bk][:, :], xn[:, c, b * 128:(b + 1) * 128], sv[:, c, :], c == 0, c == KC - 1,
                           [('wsl', si), ('xn', c, tix)], [('ps', bk)])
                    t = nxt('t32', 4)
                    ACT(t32[t][:, :], ps[bk][:, :], AF.Copy, [('ps', bk)], [('t32', t)], tag='k')
                    ob = b - (grp['ntok'] // 128 - 4)
                    dma('sp', klast_d[half, ob * 128:(ob + 1) * 128, :], t32[t][:, :], [('t32', t)], [], chan=f"t32_{t}")
            for sti in stiles:
                soff = tiles[sti][0]
                for s_ in range(2):
                    bk = nxt('mm', 4)
                    for c in range(KC):
                        MM(ps[bk][0:16, :], xn[:, c, soff + s_ * 16:soff + (s_ + 1) * 16], sv[:, c, :], c == 0, c == KC - 1,
                           [('wsl', si), ('xn', c, sti)], [('ps', bk)])
                    t = nxt('t32', 4)
                    ACT(t32[t][0:16, :], ps[bk][0:16, :], AF.Copy, [('ps', bk)], [('t32', t)])
                    dma('sp', ksam_d[half, s_ * 16:(s_ + 1) * 16, :], t32[t][0:16, :], [('t32', t)], [('ksam', half, s_)], chan=f"t32_{t}")
        for half in range(2):
            si, sv = load_cols(w_kv_d, [(D + half * 512, 512)])
            if ptiles:
                nb = grp['ntok'] // 128
                for b in range(nb):
                    tix = [i for i, (o_, w_) in enumerate(tiles) if o_ <= b * 128 < o_ + w_][0]
                    bk = nxt('mm', 4)
                    for c in range(KC):
                        MM(ps[bk][:, :], xn[:, c, b * 128:(b + 1) * 128], sv[:, c, :], c == 0, c == KC - 1,
                           [('wsl', si), ('xn', c, tix)], [('ps', bk)])
                    slot = (grp['blk0'] + b) % RING
                    if not (grp['kind'] == 'main' and grp['last'] and b >= nb - 4):
                        CP(vr[:, slot, half * 512:(half + 1) * 512], ps[bk][:, :], [('ps', bk)], [('vr', slot, half)])
                    else:
                        t = nxt('t32', 4)
                        ACT(t32[t][:, :], ps[bk][:, :], AF.Copy, [('ps', bk)], [('t32', t)], tag='v')
                        CP(vr[:, slot, half * 512:(half + 1) * 512], t32[t][:, :], [('t32', t)], [('vr', slot, half)])
                        ob = b - (nb - 4)
                        dma('sp', vlast_d[half, ob * 128:(ob + 1) * 128, :], t32[t][:, :], [('t32', t)], [], chan=f"t32_{t}")
            for sti in stiles:
                soff = tiles[sti][0]
                for s_ in range(2):
                    bk = nxt('mm', 4)
                    for c in range(KC):
                        MM(ps[bk][0:16, :], xn[:, c, soff + s_ * 16:soff + (s_ + 1) * 16], sv[:, c, :], c == 0, c == KC - 1,
                           [('wsl', si), ('xn', c, sti)], [('ps', bk)])
                    t = nxt('t32', 4)
                    ACT(t32[t][0:16, :], ps[bk][0:16, :], AF.Copy, [('ps', bk)], [('t32', t)])
                    dma('sp', vsam_d[half, s_ * 16:(s_ + 1) * 16, :], t32[t][0:16, :], [('t32', t)], [('vsam', half, s_)], chan=f"t32_{t}")

    def attn_layer(jl, grp):
        tiles = grp['tiles']
        samp = grp['kind'] == 'samp'
        rmsnorm(tiles, ('norm_attn', jl), xn, 'xn')
        for half in range(2):
            si, sv = load_cols(w_q_d[jl], [(half * 512, 512)])
            for ti, (off, w) in enumerate(tiles):
                for jj in range(4):
                    j = half * 4 + jj
                    bk = nxt('mm', 4)
                    for c in range(KC):
                        MM(ps[bk][:, 0:w], sv[:, c, jj * 128:(jj + 1) * 128], xn[:, c, off:off + w], c == 0, c == KC - 1,
                           [('wsl', si), ('xn', c, ti)], [('ps', bk)])
                    ACT(QT[:, j, off:off + w], ps[bk][:, 0:w], AF.Copy, [('ps', bk), SCR], [('qt', j, ti), SCR], scale=HD ** -0.5)
        for hq in range(0, NH, 4):
            dma('sp', biasT[:, hq:hq + 4, :], btab_d[jl, hq:hq + 4, :, :].rearrange("h q k -> q h k"),
                [('qt', KC - 1, len(tiles) - 1)], ['biasT'], chan='bias', multi=(hq > 0))
        MEMSET(biasT[0:64, :, 192:256], NEG, ['biasT'])
        qgroups = []
        if not samp:
            for b in range(grp['ntok'] // 128):
                gb = grp['blk0'] + b
                qgroups.append(dict(q0=b * 128, nq=128, kblocks=[((gb - 4 + i) % RING, 128, grp['first'] and (gb - 4 + i) <= 4) for i in range(5)]))
        else:
            for s_ in range(2):
                qgroups.append(dict(q0=16 * s_, nq=16, kblocks=[(5 * s_ + i, 128 if i < 4 else 16, False) for i in range(5)]))
        items = []
        for qg in qgroups:
            q0, nq = qg['q0'], qg['nq']
            tix = [i for i, (o_, w_) in enumerate(tiles) if o_ <= q0 < o_ + w_][0]
            ncols = sum(kw for (_, kw, _) in qg['kblocks'])
            oi = nxt('otok', 2)
            for hd_ in range(NH):
                items.append(dict(q0=q0, nq=nq, tix=tix, ncols=ncols, oi=oi, hd=hd_, kb=qg['kblocks'], n=len(items)))
        NST = 8

        def stA(it):
            n, nq, hd_, q0 = it['n'], it['nq'], it['hd'], it['q0']
            cj, pb = hd_ // 2, (hd_ % 2) * 64
            sA, sB = (0, 1) if n % 2 == 0 else (2, 3)
            qT = QT[pb:pb + 64, cj, q0:q0 + nq]
            kb = it['kb']
            merged = (not any(hm for (_, _, hm) in kb[0:4])) and all(kb[i][0] == kb[0][0] + i and kb[i][1] == 128 for i in range(4))
            for i, (slot, kw, hm) in enumerate(kb):
                if merged and i < 4:
                    if i == 0:
                        MM(ps[sA][0:nq, 0:512], qT, ktr[pb:pb + 64, cj, slot * 128:slot * 128 + 512], True, True,
                           [('qt', cj, it['tix'])] + [('ktr', cj, slot + i2) for i2 in range(4)], [('ps', sA)])
                    continue
                outp = ps[sA][0:nq, i * 128:i * 128 + kw] if i < 4 else ps[sB][0:nq, 0:kw]
                okey = ('ps', sA) if i < 4 else ('ps', sB)
                MM(outp, qT, ktr[pb:pb + 64, cj, slot * 128:slot * 128 + kw], True, not hm,
                   [('qt', cj, it['tix']), ('ktr', cj, slot)], [okey])
                if hm:
                    MM(outp, ones_b[0:1, 0:nq], maskrow[0:1, 0:kw], False, True, ['ones_b', 'maskrow'], [okey])

        def stB(it, git=None):
            n, nq, hd_, ncols = it['n'], it['nq'], it['hd'], it['ncols']
            sA, sB = (0, 1) if n % 2 == 0 else (2, 3)
            sbi, st = n % 2, n % NST
            sbv = Sb[sbi]
            mx = small[:, 8 * st:8 * st + 3]
            TS(sbv[0:nq, 0:64], ps[sA][0:nq, 0:64], cA[0:nq, jl, hd_:hd_ + 1], None, ALU.add, ALU.max,
               [('ps', sA), ('cA', jl)], [('sb', sbi), ('mx', st)], accum_out=mx[0:nq, 0:1])
            TS(sbv[0:nq, 64:384], ps[sA][0:nq, 64:384], vecs[0:nq, VCOL[('cB', jl)] + hd_:VCOL[('cB', jl)] + hd_ + 1], None,
               ALU.add, ALU.max, [('ps', sA), 'vecs'], [('sb', sbi), ('mx', st)], accum_out=mx[0:nq, 1:2])
            TT(sbv[0:nq, 384:512], ps[sA][0:nq, 384:512], biasT[0:nq, hd_, 0:128], ALU.add, [('ps', sA), 'biasT'], [('sb3', sbi)])
            nb2 = ncols - 512
            TT(sbv[0:nq, 512:ncols], ps[sB][0:nq, 0:nb2], biasT[0:nq, hd_, 128:128 + nb2], ALU.add, [('ps', sB), 'biasT'], [('sb4', sbi)])
            if git is not None:
                gst = git['n'] % NST
                gnq = git['nq']
                RECIP(small[0:gnq, 8 * gst + 5:8 * gst + 6], small[0:gnq, 8 * gst + 4:8 * gst + 5], [('rs', gst)], [('ri', gst)])
            S.op('dve', lambda hh, a=mx[0:nq, 2:3], b_=sbv[0:nq, 384:ncols]: hh.reduce_max(a, b_, AX.X),
                 [('sb3', sbi), ('sb4', sbi)], [('mx2', st)])
            ng = small[:, 8 * st + 3:8 * st + 4]
            S.op('dve', lambda hh, a=ng[0:nq, :], b_=mx[0:nq, :]: hh.tensor_reduce(a, b_, AX.X, ALU.max, negate=True),
                 [('mx', st), ('mx2', st)], [('ng', st)])

        def stC(it):
            n, nq, ncols = it['n'], it['nq'], it['ncols']
            sbi, st = n % 2, n % NST
            ng = small[:, 8 * st + 3:8 * st + 4]
            rs = small[:, 8 * st + 4:8 * st + 5]
            ACT(Pb[sbi][0:nq, 0:ncols], Sb[sbi][0:nq, 0:ncols], AF.Exp, [('sb', sbi), ('sb3', sbi), ('sb4', sbi), ('ng', st)], [('pb', sbi), ('rs', st)],
                bias=ng[0:nq, :], accum_out=rs[0:nq, :])

        def stD(it):
            n, nq = it['n'], it['nq']
            sbi = n % 2
            tb_ = 4 + sbi
            ptv = ps[tb_][:, :].bitcast(BF16)
            for i, (slot, kw, hm) in enumerate(it['kb']):
                TR(ptv[0:kw, i * 128:i * 128 + nq], Pb[sbi][0:nq, i * 128:i * 128 + kw], ident_b[0:nq, 0:nq],
                   [('pb', sbi), 'ident_b'], [('ps', tb_)])

        def stE(it):
            n, nq = it['n'], it['nq']
            tb_ = 4 + n % 2
            p3 = n % 3
            ptv = ps[tb_][:, :].bitcast(BF16)
            ACT(PT[p3][:, :, 0:nq], ptv[:, 0:640].rearrange("p (b n) -> p b n", b=5)[:, :, 0:nq], AF.Copy,
                [('ps', tb_)], [('pt', p3)])

        def stF(it):
            n, nq, hd_ = it['n'], it['nq'], it['hd']
            p3 = n % 3
            ob_ = 6 + (hd_ % 2)
            oc = (hd_ // 2) * 64
            for i, (slot, kw, hm) in enumerate(it['kb']):
                MM(ps[ob_][0:nq, oc:oc + 64], PT[p3][0:kw, i, 0:nq], vr[0:kw, slot, hd_ * 64:(hd_ + 1) * 64], i == 0, i == 4,
                   [('pt', p3), ('vr', slot, hd_ // 8)], [('ps', ob_)])

        def stG(it):
            n, nq, hd_, oi, q0 = it['n'], it['nq'], it['hd'], it['oi'], it['q0']
            st = n % NST
            ob_ = 6 + (hd_ % 2)
            oc = (hd_ // 2) * 64
            rs = small[:, 8 * st + 4:8 * st + 5]
            ri = small[:, 8 * st + 5:8 * st + 6]
            ACT(Otok[oi][0:nq, hd_ * 64:(hd_ + 1) * 64], ps[ob_][0:nq, oc:oc + 64], AF.Copy, [('ps', ob_), ('ri', st)],
                [('otok', oi, hd_)], scale=ri[0:nq, :])
            if hd_ == NH - 1:
                tb_ = 4 + (n + 1) % 2
                otv = ps[tb_][:, :].bitcast(BF16)
                for c in range(KC):
                    TR(otv[:, c * 128:c * 128 + nq], Otok[oi][0:nq, c * 128:(c + 1) * 128], ident_b[0:nq, 0:nq],
                       [('otok', oi, 2 * c), ('otok', oi, 2 * c + 1), 'ident_b'], [('ps', tb_)])
                CP(xn[:, :, q0:q0 + nq], otv[:, :].rearrange("p (c n) -> p c n", c=8)[:, :, 0:nq], [('ps', tb_)],
                   [('xn', c, it['tix']) for c in range(KC)])

        NI = len(items)
        for s_ in range(NI + 4):
            if s_ < NI:
                stA(items[s_])
            if 0 <= s_ - 2 < NI:
                stD(items[s_ - 2])
                stE(items[s_ - 2])
            if 0 <= s_ - 3 < NI:
                stF(items[s_ - 3])
            if 0 <= s_ - 4 < NI:
                git = items[s_ - 4]
                gst = git['n'] % NST
                RECIP(small[0:git['nq'], 8 * gst + 5:8 * gst + 6], small[0:git['nq'], 8 * gst + 4:8 * gst + 5], [('rs', gst)], [('ri', gst)])
                stG(git)
            if s_ < NI:
                stB(items[s_], None)
                stC(items[s_])
        for half in range(2):
            si, sv = load_cols(w_o_d[jl], [(half * 512, 512)])
            for ti, (off, w) in enumerate(tiles):
                for jj in range(4):
                    j = half * 4 + jj
                    bk = nxt('mm', 4)
                    for c in range(KC):
                        MM(ps[bk][:, 0:w], sv[:, c, jj * 128:(jj + 1) * 128], xn[:, c, off:off + w], c == 0, c == KC - 1,
                           [('wsl', si), ('xn', c, ti)], [('ps', bk)])
                    TT(h[:, j, off:off + w], ps[bk][:, 0:w], h[:, j, off:off + w], ALU.add, [('ps', bk), ('h', j, ti)], [('h', j, ti)])

    def final_out(grp, out_ap):
        tiles = grp['tiles']
        rmsnorm(tiles, 'norm_final', ynf, 'ynf')
        for ti, (off, w) in enumerate(tiles):
            nb = max(1, w // 128)
            for b in range(nb):
                bw = min(128, w)
                si = nxt('stg', 2)
                for half in range(2):
                    bk = nxt('mm', 4)
                    for cc in range(4):
                        c = half * 4 + cc
                        TR(ps[bk][0:bw, cc * 128:(cc + 1) * 128], ynf[:, c, off + b * 128:off + b * 128 + bw], ident_f[:, :],
                           [('ynf', c, ti), 'ident_f'], [('ps', bk)])
                    if half == 0:
                        ACT(stg[si][0:bw, 0:512], ps[bk][0:bw, :], AF.Copy, [('ps', bk)], [('stg', si)])
                    else:
                        CP(stg[si][0:bw, 512:1024], ps[bk][0:bw, :], [('ps', bk)], [('stg', si)])
                dma('sp', out_ap[off + b * 128:off + b * 128 + bw, :], stg[si][0:bw, :], [('stg', si)], [], chan=f"stg{si}")

    def conv_out():
        for l in range(2):
            si = nxt('stg', 2)
            for half in range(2):
                bk = nxt('mm', 4)
                for cc in range(4):
                    c = half * 4 + cc
                    TR(ps[bk][0:CH, cc * 128:(cc + 1) * 128], hist[:, l, c, :], ident_f[:, :], ['hist', ('histw', c), 'ident_f'], [('ps', bk)])
                CP(stg[si][0:CH, half * 512:(half + 1) * 512], ps[bk][0:CH, :], [('ps', bk)], [('stg', si)])
            dma('sp', convp_d[l, :, :], stg[si][0:CH, :], [('stg', si)], [], chan=f"stg{si}")

    def load_cache():
        for s_ in range(2):
            for b in range(4):
                dma('pool', vr[:, 5 * s_ + b, :], cv_d[s_, b * 128:(b + 1) * 128, :], (),
                    [('vr', 5 * s_ + b, 0), ('vr', 5 * s_ + b, 1)], chan='cv', multi=True)
                si = nxt('stg', 2)
                dma('sp', stg[si][:, :], ck_d[s_, b * 128:(b + 1) * 128, :], (), [('stg', si)], chan=f"stg{si}")
                for half in range(2):
                    bk = nxt('mm', 4)
                    for cc in range(4):
                        c = half * 4 + cc
                        TR(ps[bk][:, cc * 128:(cc + 1) * 128], stg[si][:, c * 128:(c + 1) * 128], ident_f[:, :],
                           [('stg', si), 'ident_f'], [('ps', bk)])
                    o3 = ktr[:, half * 4:half * 4 + 4, (5 * s_ + b) * 128:(5 * s_ + b + 1) * 128]
                    i3 = ps[bk][:, :].rearrange("p (c n) -> p c n", c=4)
                    CP(o3, i3, [('ps', bk)], [('ktr', half * 4 + cc, 5 * s_ + b) for cc in range(4)])
            slot = 5 * s_ + 4
            for half in range(2):
                dma('pool', vr[0:16, slot, half * 512:(half + 1) * 512], vsam_d[half, s_ * 16:(s_ + 1) * 16, :],
                    [('vsam', half, s_)], [('vr', slot, half)], chan='cv', multi=True)
            si = nxt('stg', 2)
            for half in range(2):
                dma('sp', stg[si][0:16, half * 512:(half + 1) * 512], ksam_d[half, s_ * 16:(s_ + 1) * 16, :],
                    [('ksam', half, s_)], [('stg', si)], chan=f"stg{si}", multi=(half > 0))
            for half in range(2):
                bk = nxt('mm', 4)
                for cc in range(4):
                    c = half * 4 + cc
                    TR(ps[bk][:, cc * 128:cc * 128 + 16], stg[si][0:16, c * 128:(c + 1) * 128], ident_f[0:16, 0:16],
                       [('stg', si), 'ident_f'], [('ps', bk)])
                o3 = ktr[:, half * 4:half * 4 + 4, slot * 128:slot * 128 + 16]
                i3 = ps[bk][:, :].rearrange("p (c n) -> p c n", c=4)[:, :, 0:16]
                CP(o3, i3, [('ps', bk)], [('ktr', half * 4 + cc, slot) for cc in range(4)])

    groups = [
        dict(kind='halo', tiles=[(0, 256), (256, 384), (640, 32)], stiles=(2,), ntok=640, blk0=0, x0=0, last=False),
        dict(kind='main', tiles=[(0, 512), (512, 512)], ntok=1024, blk0=5, x0=640, last=False, first=True),
        dict(kind='main', tiles=[(0, 512), (512, 512)], ntok=1024, blk0=5, x0=1664, last=True, first=False),
        dict(kind='samp', tiles=[(0, 32)], ntok=32, blk0=None, x0=0, last=False),
    ]
    for gi, grp in enumerate(groups):
        if group_sel is not None and gi not in group_sel:
            continue
        if grp['kind'] == 'samp':
            conv_out()
            load_cache()
            CP(h[:, :, 0:32], h_s[:, :, :], ['h_s'], [('h', c, 0) for c in range(KC)])
        elif grp['kind'] == 'halo':
            load_x(x_d[0:640, :], grp['tiles'][0:2])
            load_x(xs_d, grp['tiles'][2:3], src_base=640, ti0=2)
        else:
            if grp['kind'] == 'main' and not grp['first']:
                ACT(ktr[:, :, 128:640], ktr[:, :, 9 * 128:13 * 128], AF.Copy,
                    [('ktr', j, s_) for j in range(KC) for s_ in range(9, 13)],
                    [('ktr', j, s_) for j in range(KC) for s_ in range(1, 5)])
                CP(vr[:, 1:5, :], vr[:, 9:13, :], [('vr', s_, hf) for s_ in range(9, 13) for hf in range(2)],
                   [('vr', s_, hf) for s_ in range(1, 5) for hf in range(2)])
            load_x(x_d[grp['x0']:grp['x0'] + grp['ntok'], :], grp['tiles'])
        if grp['kind'] != 'samp':
            for l in range(2):
                conv_layer(l, grp)
                ffn_layer(l, grp)
            kv_stage(grp)
        if grp['kind'] == 'halo':
            CP(h_s[:, :, :], h[:, :, 640:672], [('h', c, 2) for c in range(KC)], ['h_s'])
            continue
        for jl in range(2):
            attn_layer(jl, grp)
            ffn_layer(2 + jl, grp)
        if grp['kind'] == 'samp':
            final_out(grp, ys_d)
        else:
            final_out(grp, y_d[(gi - 1) * 1024:gi * 1024, :])

    S.finalize()
    sems = {e: es.enter_context(nc.semaphore(f"sem_{e}")) for e in ENGS}
    csems = {name: es.enter_context(nc.semaphore(f"c_{name}")) for name in S.chan_cnt}
    with nc.Block() as block:
        S.emit(nc, block, sems, csems)
    es.close()
    return nc


_CACHE = {}


def _layout_vec(v):
    return np.ascontiguousarray(np.asarray(v, np.float32).reshape(-1, 128).T)


def kernel(x_prompt, x_sample, cache_conv, cache_k, cache_v,
           norm_conv, w_pw1, b_pw1, w_dw, b_dw, ln_g, ln_b, w_pw2, b_pw2,
           norm_kv, w_kv, norm_attn, w_q, w_o, rel_bias,
           norm_ffn, w_ffn_in, w_ffn_out, norm_final):
    f = lambda a: np.ascontiguousarray(np.asarray(a, dtype=np.float32))
    x_prompt, x_sample, cache_conv, cache_k, cache_v = map(f, (x_prompt, x_sample, cache_conv, cache_k, cache_v))
    if 'nc' not in _CACHE:
        _CACHE['nc'] = build_program()
    nc = _CACHE['nc']
    vecs = np.zeros((128, NV), np.float32)

    def put(name, v):
        a = _layout_vec(v)
        vecs[:, VCOL[name]:VCOL[name] + a.shape[1]] = a
    rel_bias = f(rel_bias)
    for l in range(2):
        put(('norm_conv', l), norm_conv[l])
        put(('b_pw1a', l), np.asarray(b_pw1)[l, :D])
        put(('b_pw1g', l), np.asarray(b_pw1)[l, D:])
        put(('b_dw', l), b_dw[l])
        put(('ln_g', l), ln_g[l])
        put(('ln_b', l), ln_b[l])
        put(('b_pw2', l), b_pw2[l])
        put(('norm_attn', l), norm_attn[l])
        vecs[:, VCOL[('cB', l)]:VCOL[('cB', l)] + NH] = np.broadcast_to(rel_bias[l, :, 256][None, :], (128, NH))
    for l in range(4):
        put(('norm_ffn', l), norm_ffn[l])
    put('norm_kv', norm_kv)
    put('norm_final', norm_final)
    wdw = np.ascontiguousarray(f(w_dw).reshape(2, CW, KC, 128).transpose(3, 0, 2, 1)).reshape(128, 2 * KC * CW)
    idx = np.clip(np.arange(128)[:, None] - np.arange(256)[None, :] + 256, 0, 256)
    btab = np.ascontiguousarray(rel_bias[:, :, idx])
    shared = dict(w_pw1=f(w_pw1), w_pw2=f(w_pw2), w_kv=f(w_kv), w_q=f(w_q), w_o=f(w_o), w_ffn_in=f(w_ffn_in),
                  w_ffn_out=f(w_ffn_out), vecs=vecs, wdw=wdw, btab=btab,
                  ident=np.eye(128, dtype=np.float32))
    in_maps = []
    for c in range(NCORE):
        b, half = c // 2, c % 2
        xs = np.zeros((NX, D), np.float32)
        if half == 0:
            xs[HALO:] = x_prompt[b, 0:NTOK]
        else:
            xs[:] = x_prompt[b, NTOK - HALO:2 * NTOK]
        m = dict(shared)
        m['x'] = xs
        m['xs'] = np.ascontiguousarray(x_sample[2 * c:2 * c + 2].reshape(32, D))
        m['cconv'] = np.ascontiguousarray(cache_conv[:, 2 * c:2 * c + 2])
        m['ck'] = np.ascontiguousarray(cache_k[2 * c:2 * c + 2].reshape(2, 512, D))
        m['cv'] = np.ascontiguousarray(cache_v[2 * c:2 * c + 2].reshape(2, 512, D))
        m['flag'] = np.full((128, 1), float(half), np.float32)
        in_maps.append(m)
    res = run_bass_kernel_spmd(nc, in_maps, core_ids=list(range(NCORE)))
    R = res.results
    B = 4
    y_prompt = np.zeros((B, 2 * NTOK, D), np.float32)
    y_sample = np.zeros((16, 16, D), np.float32)
    conv_prompt = np.zeros((2, B, CH, D), np.float32)
    conv_sample = np.zeros((2, 16, CH, D), np.float32)
    k_prompt = np.zeros((B, 512, NH, HD), np.float32)
    v_prompt = np.zeros((B, 512, NH, HD), np.float32)
    k_sample = np.zeros((16, 16, NH, HD), np.float32)
    v_sample = np.zeros((16, 16, NH, HD), np.float32)
    for c in range(NCORE):
        b, half = c // 2, c % 2
        r = R[c]
        y_prompt[b, half * NTOK:(half + 1) * NTOK] = r['y']
        y_sample[2 * c:2 * c + 2] = r['ys'].reshape(2, 16, D)
        conv_sample[:, 2 * c:2 * c + 2] = r['convs']
        k_sample[2 * c:2 * c + 2] = r['ksam'].transpose(1, 0, 2).reshape(2, 16, NH, HD)
        v_sample[2 * c:2 * c + 2] = r['vsam'].transpose(1, 0, 2).reshape(2, 16, NH, HD)
        if half == 1:
            conv_prompt[:, b] = r['convp']
            k_prompt[b] = r['klast'].transpose(1, 0, 2).reshape(512, NH, HD)
            v_prompt[b] = r['vlast'].transpose(1, 0, 2).reshape(512, NH, HD)
    return (y_prompt, y_sample, conv_prompt, conv_sample, k_prompt, v_prompt, k_sample, v_sample)
```

```python
import numpy as np
import concourse.bass as bass
import concourse.mybir as mybir
from concourse.bass_utils import run_bass_kernel_spmd

F32 = mybir.dt.float32
F32R = mybir.dt.float32r
BF16 = mybir.dt.bfloat16
AF = mybir.ActivationFunctionType
ALU = mybir.AluOpType
AX = mybir.AxisListType

D = 1024
KC = 8
FF = 2816
FC = 22
NH = 16
HD = 64
CW = 31
CH = 30
NEG = -1.0e30
EPS = 1e-6
NCORE = 8
NTOK = 2048
HALO = 640
NX = NTOK + HALO
RING = 13
ENGS = ['pe', 'act', 'dve', 'pool', 'sp']
BLK = {'pe': 'tensor', 'act': 'scalar', 'dve': 'vector', 'pool': 'gpsimd', 'sp': 'sync'}


class Sched:
    def __init__(s):
        s.ops = {e: [] for e in ENGS}
        s.lastw = {}
        s.rds = {}
        s.chan_cnt = {}

    def op(s, eng, fn, r=(), w=(), chan=None, multi=False):
        idx = len(s.ops[eng])
        deps = {}

        def add(ref):
            if ref is None:
                return
            k, v = ref
            if deps.get(k, -1) < v:
                deps[k] = v
        r = [k_ for k_ in r if k_ != ('scr',)]
        w = [k_ for k_ in w if k_ != ('scr',)]
        for key in r:
            add(s.lastw.get(key))
        for key in w:
            add(s.lastw.get(key))
            for ref in s.rds.get(key, ()):
                add(ref)
        if chan is not None:
            c = s.chan_cnt.get(chan, 0) + 1
            s.chan_cnt[chan] = c
            ref = (('c', chan), c)
            if multi:
                deps.pop(('c', chan), None)
        else:
            ref = (('e', eng), idx)
        if eng == 'pe':
            deps.pop(('e', 'pe'), None)
        o = dict(fn=fn, deps=deps, sig=False, chan=chan, val=None)
        s.ops[eng].append(o)
        for key in w:
            s.lastw[key] = ref
            s.rds[key] = []
        for key in r:
            s.rds.setdefault(key, []).append(ref)
        return o

    def finalize(s):
        for e in ENGS:
            for o in s.ops[e]:
                for (k, name), v in o['deps'].items():
                    if k == 'e':
                        s.ops[name][v]['sig'] = True
        for e in ENGS:
            c = 0
            for o in s.ops[e]:
                if o['chan'] is None and o['sig']:
                    c += 1
                    o['val'] = c

    def emit(s, nc, block, sems, csems):
        for e in ENGS:
            if not s.ops[e]:
                continue

            def body(h, e=e):
                waited = {}
                for o in s.ops[e]:
                    for (k, name), v in o['deps'].items():
                        if k == 'e':
                            sem = sems[name]
                            val = s.ops[name][v]['val']
                        else:
                            sem = csems[name]
                            val = 16 * (s.chan_cnt[name] if name in ('init', 'cv') else v)
                        if waited.get((k, name), 0) >= val:
                            continue
                        waited[(k, name)] = val
                        h.wait_ge(sem, val)
                    ins = o['fn'](h)
                    if o['chan'] is not None:
                        ins.then_inc(csems[o['chan']], 16)
                    elif o['sig']:
                        ins.then_inc(sems[e], 1)
                if e == 'sp':
                    for name, cnt in s.chan_cnt.items():
                        h.wait_ge(csems[name], 16 * cnt)
            getattr(block, BLK[e])(body)


def vec_cols():
    cols = {}
    n = 0

    def add(name, k):
        nonlocal n
        cols[name] = n
        n += k
    for l in range(2):
        add(('norm_conv', l), 8)
        add(('b_pw1a', l), 8)
        add(('b_pw1g', l), 8)
        add(('b_dw', l), 8)
        add(('ln_g', l), 8)
        add(('ln_b', l), 8)
        add(('b_pw2', l), 8)
        add(('norm_attn', l), 8)
        add(('cB', l), 16)
    for l in range(4):
        add(('norm_ffn', l), 8)
    add('norm_kv', 8)
    add('norm_final', 8)
    return cols, n


VCOL, NV = vec_cols()


def build_program(group_sel=None):
    nc = bass.Bass("TRN2", target_bir_lowering=False)
    S = Sched()

    def din(name, shape):
        return nc.dram_tensor(name, shape, F32, kind="ExternalInput")

    def dout(name, shape):
        return nc.dram_tensor(name, shape, F32, kind="ExternalOutput")

    x_d = din("x", [NX, D]).ap()
    xs_d = din("xs", [32, D]).ap()
    cconv_d = din("cconv", [2, 2, CH, D]).ap()
    ck_d = din("ck", [2, 512, D]).ap()
    cv_d = din("cv", [2, 512, D]).ap()
    w_pw1_d = din("w_pw1", [2, D, 2 * D]).ap()
    w_pw2_d = din("w_pw2", [2, D, D]).ap()
    w_kv_d = din("w_kv", [D, 2 * D]).ap()
    w_q_d = din("w_q", [2, D, D]).ap()
    w_o_d = din("w_o", [2, D, D]).ap()
    w_in_d = din("w_ffn_in", [4, D, 2 * FF]).ap()
    w_out_d = din("w_ffn_out", [4, FF, D]).ap()
    vecs_d = din("vecs", [128, NV]).ap()
    wdw_d = din("wdw", [128, 2 * KC * CW]).ap()
    btab_d = din("btab", [2, NH, 128, 256]).ap()
    flag_d = din("flag", [128, 1]).ap()
    ident_d = din("ident", [128, 128]).ap()

    y_d = dout("y", [NTOK, D]).ap()
    ys_d = dout("ys", [32, D]).ap()
    convp_d = dout("convp", [2, CH, D]).ap()
    convs_d = dout("convs", [2, 2, CH, D]).ap()
    klast_d = dout("klast", [2, 512, 512]).ap()
    vlast_d = dout("vlast", [2, 512, 512]).ap()
    ksam_d = dout("ksam", [2, 32, 512]).ap()
    vsam_d = dout("vsam", [2, 32, 512]).ap()

    from contextlib import ExitStack
    es = ExitStack()

    def sb(name, shape, dt):
        return es.enter_context(nc.sbuf_tensor("sb_" + name, shape, dt))

    GM = 1024
    h = sb("h", [128, KC, GM], F32)
    xn = sb("xn", [128, KC, GM], BF16)
    NSCR = 8 * (GM + CH) + 8 * 512
    scr = sb("scr", [128, NSCR], F32)
    ktr = sb("ktr", [128, KC, RING * 128], BF16)
    vr = sb("vr", [128, RING, D], BF16)
    NSLOT = 3
    wsl = [sb(f"wsl{i}", [128, 4096], BF16) for i in range(NSLOT)]
    vecs = sb("vecs", [128, NV], F32)
    wdw = sb("wdw", [128, 2 * KC * CW], F32)
    flag = sb("flag", [128, 1], F32)
    hist = sb("hist", [128, 2, KC, CH], F32)
    ones_r = sb("ones_r", [128, 128], F32R)
    ones_f = sb("ones_f", [128, 128], F32)
    ident_f = sb("ident_f", [128, 128], F32)
    ident_b = sb("ident_b", [128, 128], BF16)
    ones_b = sb("ones_b", [1, 128], BF16)
    maskrow = sb("maskrow", [1, 128], BF16)
    epst = sb("epst", [128, 1], F32)
    cA = sb("cA", [128, 2, NH], F32)
    rowmask = sb("rowmask", [128, 1], F32)
    sq = [sb(f"sq{i}", [128, 512], F32R) for i in range(4)]
    t32 = [sb(f"t32_{i}", [128, 512], F32) for i in range(4)]
    stg = [sb(f"stg{i}", [128, D], F32) for i in range(2)]
    small = sb("small", [128, 64], F32)
    h_s = sb("h_s", [128, KC, 32], F32)
    NDG = 8
    dg = [sb(f"dg{i}", [128, 128], BF16) for i in range(NDG)]
    assert 11072 + 1024 <= NSCR
    ps = [es.enter_context(nc.psum_tensor(f"ps{i}", [128, 512], F32)) for i in range(8)]

    full = scr[:, 0:8 * (GM + CH)].rearrange("p (c n) -> p c n", c=8)
    FBW = GM + 32
    fullb = scr[:, 0:4 * FBW].bitcast(BF16).rearrange("p (c n) -> p c n", c=8)
    fulls = scr[:, 4 * FBW:4 * FBW + 8 * 92].rearrange("p (c n) -> p c n", c=8)
    yc = scr[:, 8 * (GM + CH):8 * (GM + CH) + 4096].rearrange("p (c n) -> p c n", c=8)
    gbuf = scr[:, 0:11264].bitcast(BF16).rearrange("p (f n) -> p f n", f=FC)
    QT = scr[:, 0:4096].bitcast(BF16).rearrange("p (c n) -> p c n", c=8)
    biasT = scr[:, 4096:8192].rearrange("p (h n) -> p h n", h=NH)
    Sb = [scr[:, 8192 + i * 640:8192 + (i + 1) * 640] for i in range(2)]
    Pb = [scr[:, 9472 + i * 320:9472 + (i + 1) * 320].bitcast(BF16) for i in range(2)]
    PT = [scr[:, 10112 + i * 320:10112 + (i + 1) * 320].bitcast(BF16).rearrange("p (b n) -> p b n", b=5) for i in range(3)]
    Otok = [scr[:, 11072 + i * 512:11072 + (i + 1) * 512].bitcast(BF16) for i in range(2)]
    ynf = scr[:, 0:8192].rearrange("p (c n) -> p c n", c=8)

    scr_state = {'stage': None, 'fence': []}

    def scr_keys(stage):
        return ('scrstage',)

    V = lambda name, c=None: vecs[:, VCOL[name] + (0 if c is None else c):VCOL[name] + (0 if c is None else c) + 1]

    def dma(q, out, in_, r, w, chan, multi=False, **kw):
        S.op(q, lambda hh: hh.dma_start(out=out, in_=in_, **kw), r, w, chan=chan, multi=multi)

    def MM(out, lhsT, rhs, start, stop, r, w):
        S.op('pe', lambda hh: hh.matmul(out, lhsT, rhs, start=start, stop=stop), r, w)

    def TR(out, in_, ident, r, w):
        S.op('pe', lambda hh: hh.transpose(out, in_, ident), r, w)

    def ACT(out, in_, func, r, w, bias=None, scale=None, accum_out=None, tag=None):
        kw = {}
        if bias is not None:
            kw['bias'] = bias
        if scale is not None:
            kw['scale'] = scale
        if accum_out is not None:
            kw['accum_out'] = accum_out
        S.op('act', lambda hh: hh.activation(out, in_, func, **kw), r, w)

    def TT(out, in0, in1, op, r, w, eng='dve'):
        S.op(eng, lambda hh: hh.tensor_tensor(out, in0, in1, op), r, w)

    def TS(out, in0, s1, s2, op0, op1, r, w, accum_out=None, eng='dve'):
        if op1 is None:
            S.op(eng, lambda hh: hh.tensor_scalar(out, in0, s1, s2, op0, accum_out=accum_out), r, w)
        else:
            S.op(eng, lambda hh: hh.tensor_scalar(out, in0, s1, s2, op0, op1, accum_out=accum_out), r, w)

    def STT(out, in0, sc, in1, op0, op1, r, w):
        S.op('dve', lambda hh: hh.scalar_tensor_tensor(out, in0, sc, in1, op0, op1), r, w)

    def CP(out, in_, r, w, eng='dve'):
        S.op(eng, lambda hh: hh.tensor_copy(out, in_), r, w)

    def MEMSET(ap, val, w, eng='dve'):
        S.op(eng, lambda hh: hh.memset(ap, val), (), w)

    def RECIP(out, in_, r, w):
        S.op('dve', lambda hh: hh.reciprocal(out, in_), r, w)

    SCR = ('scr',)
    rr = {'mm': 0, 'w': 0, 'sq': 0, 't32': 0, 'stg': 0, 'otok': 0, 'dg': 0}

    def nxt(name, n):
        v = rr[name]
        rr[name] = (v + 1) % n
        return v

    def load_cols(wd, col_lists):
        si = nxt('w', NSLOT)
        sv = wsl[si][:, :].rearrange("p (c n) -> p c n", c=8)
        src = wd.rearrange("(c p) n -> p c n", p=128)
        o = 0
        for (c0, ncol) in col_lists:
            dma('pool', sv[:, :, o:o + ncol], src[:, :, c0:c0 + ncol], (), [('wsl', si)], chan=f"wsl{si}", multi=(o > 0))
            o += ncol
        return si, sv

    def load_wout(wd, j):
        si = nxt('w', NSLOT)
        sv = wsl[si][:, 0:FC * 128].rearrange("p (f n) -> p f n", f=FC)
        src = wd.rearrange("(f p) n -> p f n", p=128)
        dma('pool', sv, src[:, :, j * 128:(j + 1) * 128], (), [('wsl', si)], chan=f"wsl{si}")
        return si, sv

    dma('sp', vecs[:, :], vecs_d, (), ['vecs'], chan='init')
    dma('sp', wdw[:, :], wdw_d, (), ['wdw'], chan='init')
    dma('sp', flag[:, :], flag_d, (), ['flag'], chan='init')
    MEMSET(ones_f[:, :], 1.0 / D, ['ones_f'])
    ACT(ones_r[:, :], ones_f[:, :], AF.Copy, ['ones_f'], ['ones_r'])
    MEMSET(ones_b[:, :], 1.0, ['ones_b'])
    MEMSET(epst[:, :], EPS, ['epst'])
    MEMSET(hist[:, :, :, :], 0.0, ['hist'])
    dma('sp', ident_f[:, :], ident_d, (), ['ident_f'], chan='init')
    CP(ident_b[:, :], ident_f[:, :], ['ident_f'], ['ident_b'])
    MEMSET(rowmask[:, :], 0.0, ['rowmask'])
    MEMSET(rowmask[64:128, :], NEG, ['rowmask'])
    TS(maskrow[0:1, :], ones_b[0:1, :], flag[0:1, 0:1], -1.0, ALU.mult, ALU.add, ['ones_b', 'flag'], ['maskrow'])
    TS(maskrow[0:1, :], maskrow[0:1, :], -NEG, None, ALU.mult, None, ['maskrow'], ['maskrow'])
    for l in range(2):
        TS(cA[:, l, :], vecs[:, VCOL[('cB', l)]:VCOL[('cB', l)] + NH], rowmask[:, 0:1], None, ALU.add, None,
           ['vecs', 'rowmask'], [('cA', l)])

    def load_x(src_ap, tiles, src_base=0, ti0=0):
        for ti_, (off, w) in enumerate(tiles):
            ti = ti0 + ti_
            nb = max(1, w // 128)
            for b in range(nb):
                bw = min(128, w)
                si = nxt('stg', 2)
                dma('sp', stg[si][0:bw, :], src_ap[off - src_base + b * 128:off - src_base + b * 128 + bw, :], (), [('stg', si)], chan=f"stg{si}")
                for half in range(2):
                    bk = nxt('mm', 4)
                    for cc in range(4):
                        c = half * 4 + cc
                        TR(ps[bk][:, cc * 128:cc * 128 + bw], stg[si][0:bw, c * 128:(c + 1) * 128], ident_f[0:bw, 0:bw],
                           [('stg', si), 'ident_f'], [('ps', bk)])
                    o3 = h[:, half * 4:half * 4 + 4, off + b * 128:off + b * 128 + bw]
                    i3 = ps[bk][:, :].rearrange("p (c n) -> p c n", c=4)[:, :, 0:bw]
                    wk = [('h', half * 4 + cc, ti) for cc in range(4)]
                    if half == 0:
                        ACT(o3, i3, AF.Copy, [('ps', bk)], wk)
                    else:
                        CP(o3, i3, [('ps', bk)], wk)

    def rmsnorm(tiles, gname, dst, dkey):
        for ti, (off, w) in enumerate(tiles):
            for c in range(KC):
                qi = nxt('sq', 4)
                ACT(sq[qi][:, 0:w], h[:, c, off:off + w], AF.Square, [('h', c, ti)], [('sq', qi)])
                MM(ps[7][:, 0:w], ones_r[:, :], sq[qi][:, 0:w], c == 0, c == KC - 1, [('sq', qi), 'ones_r'], [('ps', 7)])
            ti2 = nxt('t32', 4)
            ACT(t32[ti2][:, 0:w], ps[7][:, 0:w], AF.Sqrt, [('ps', 7), 'epst'], [('t32', ti2)], bias=epst[:, 0:1])
            RECIP(t32[ti2][:, 0:w], t32[ti2][:, 0:w], [('t32', ti2)], [('t32', ti2)])
            for c in range(KC):
                STT(dst[:, c, off:off + w], h[:, c, off:off + w], V(gname, c), t32[ti2][:, 0:w], ALU.mult, ALU.mult,
                    [('h', c, ti), ('t32', ti2), 'vecs'], [(dkey, c, ti), SCR] if dkey == 'ynf' else [(dkey, c, ti)])

    def conv_layer(l, grp):
        tiles = grp['tiles']
        ntok = grp['ntok']
        stiles = grp.get('stiles', ())
        ptiles = [ti_ for ti_ in range(len(tiles)) if ti_ not in stiles]
        last_p = ptiles[-1] if ptiles else None
        SB0 = 704
        rmsnorm(tiles, ('norm_conv', l), xn, 'xn')
        if ptiles:
            CP(fullb[:, :, 0:CH], hist[:, l, :, :], ['hist'] + [('histw', j) for j in range(KC)], [('fullh',)])
        if stiles:
            for s_ in range(2):
                si = nxt('stg', 2)
                dma('sp', stg[si][0:CH, :], cconv_d[l, s_, :, :], (), [('stg', si)], chan=f"stg{si}")
                for half in range(2):
                    bk = nxt('mm', 4)
                    for cc in range(4):
                        c = half * 4 + cc
                        TR(ps[bk][:, cc * 128:cc * 128 + CH], stg[si][0:CH, c * 128:(c + 1) * 128], ident_f[0:CH, 0:CH],
                           [('stg', si), 'ident_f'], [('ps', bk)])
                    o3 = fulls[:, half * 4:half * 4 + 4, 46 * s_:46 * s_ + CH]
                    i3 = ps[bk][:, :].rearrange("p (c n) -> p c n", c=4)[:, :, 0:CH]
                    CP(o3, i3, [('ps', bk), SCR], [('fullsh',), SCR])
        for p in range(4):
            si, sv = load_cols(w_pw1_d[l], [(p * 256, 256), (D + p * 256, 256)])
            for ti, (off, w) in enumerate(tiles):
                for jj in range(2):
                    j = 2 * p + jj
                    ba = nxt('mm', 4)
                    bg = nxt('mm', 4)
                    for c in range(KC):
                        MM(ps[ba][:, 0:w], sv[:, c, jj * 128:(jj + 1) * 128], xn[:, c, off:off + w], c == 0, c == KC - 1,
                           [('wsl', si), ('xn', c, ti)], [('ps', ba)])
                    for c in range(KC):
                        MM(ps[bg][:, 0:w], sv[:, c, 256 + jj * 128:256 + (jj + 1) * 128], xn[:, c, off:off + w], c == 0, c == KC - 1,
                           [('wsl', si), ('xn', c, ti)], [('ps', bg)])
                    t = nxt('t32', 4)
                    ACT(t32[t][:, 0:w], ps[bg][:, 0:w], AF.Sigmoid, [('ps', bg), 'vecs'], [('t32', t)], bias=V(('b_pw1g', l), j))
                    if ti not in stiles:
                        dstv = fullb[:, j, CH + off:CH + off + w]
                        STT(dstv, ps[ba][:, 0:w], V(('b_pw1a', l), j), t32[t][:, 0:w], ALU.add, ALU.mult,
                            [('ps', ba), ('t32', t), 'vecs', SCR], [('full', j, ti), SCR])
                        if ti == last_p:
                            STT(hist[:, l, j, :], ps[ba][:, w - CH:w], V(('b_pw1a', l), j), t32[t][:, w - CH:w], ALU.add, ALU.mult,
                                [('ps', ba), ('t32', t), 'vecs', ('fullh',)], [('histw', j)])
                        if grp['kind'] == 'halo':
                            TS(dstv, dstv, flag[:, 0:1], None, ALU.mult, None, [('full', j, ti), 'flag'], [('full', j, ti)])
                            if ti == last_p:
                                TS(hist[:, l, j, :], hist[:, l, j, :], flag[:, 0:1], None, ALU.mult, None, [('histw', j), 'flag'], [('histw', j)])
                    else:
                        dstv = fulls[:, j, 0:92].rearrange("p (s t) -> p s t", s=2)[:, :, CH:CH + 16]
                        STT(dstv, ps[ba][:, 0:32].rearrange("p (s t) -> p s t", s=2), V(('b_pw1a', l), j),
                            t32[t][:, 0:32].rearrange("p (s t) -> p s t", s=2), ALU.add, ALU.mult,
                            [('ps', ba), ('t32', t), 'vecs', SCR], [('full', j, ti), SCR])
        allfull = [('full', j, ti_) for j in range(KC) for ti_ in stiles] + [('fullsh',)]
        if stiles:
            CP(fullb[:, :, SB0:SB0 + 92], fulls[:, :, :], allfull, [('fullsb',)])
            for s_ in range(2):
                si = nxt('stg', 2)
                for half in range(2):
                    bk = nxt('mm', 4)
                    for cc in range(4):
                        c = half * 4 + cc
                        TR(ps[bk][0:CH, cc * 128:(cc + 1) * 128], fulls[:, c, 46 * s_ + 16:46 * s_ + 46], ident_f[:, :],
                           allfull + ['ident_f'], [('ps', bk)])
                    CP(stg[si][0:CH, half * 512:(half + 1) * 512], ps[bk][0:CH, :], [('ps', bk)], [('stg', si)])
                dma('sp', convs_d[l, s_, :, :], stg[si][0:CH, :], [('stg', si)], [], chan=f"stg{si}")
        wv = wdw[:, :].rearrange("p (l c k) -> p l c k", l=2, c=KC)
        for ti, (off, w) in enumerate(tiles):
            rk = [('fullh',)] + [('full', None, t2) for t2 in range(ti + 1)]
            pending = []
            NPE = 27
            samp = ti in stiles
            for c in range(KC):
                bkD = 4 + (c % 2)
                bkP = 2 + (c % 2)
                if samp:
                    rkeys = [('fullsb',)]
                else:
                    rkeys = [('fullh',), ('full', c, ti)] + ([('full', c, ti - 1)] if (ti > 0 and (ti - 1) not in stiles) else [])

                def srcv(k, fv):
                    if not samp:
                        return fv[:, c, off + k:off + k + w]
                    return fv[:, c, SB0:SB0 + 92].rearrange("p (s t) -> p s t", s=2)[:, :, k:k + 16]

                def accv(bk):
                    if not samp:
                        return ps[bk][:, 0:w]
                    return ps[bk][:, 0:32].rearrange("p (s t) -> p s t", s=2)
                dstv = yc[:, c, 0:w] if not samp else yc[:, c, 0:32].rearrange("p (s t) -> p s t", s=2)
                for k in range(NPE):
                    di = nxt('dg', NDG)
                    if k % 2 == 0:
                        S.op('pool', lambda hh, o_=dg[di][:, :], s1=wv[:, l, c, k:k + 1]:
                             hh.tensor_scalar(o_, ident_f[:, :], s1, 0.0, ALU.mult, ALU.add), ['ident_f', 'wdw'], [('dg', di)])
                    else:
                        ACT(dg[di][:, :], ident_f[:, :], AF.Copy, ['ident_f', 'wdw'], [('dg', di)], scale=wv[:, l, c, k:k + 1])
                    MM(accv(bkP), dg[di][:, :], srcv(k, fullb), k == 0, k == NPE - 1, rkeys + [('dg', di)], [('ps', bkP)])
                for k in range(NPE, CW):
                    if k == NPE:
                        TS(accv(bkD), srcv(k, fullb), wv[:, l, c, k:k + 1], V(('b_dw', l), c), ALU.mult, ALU.add,
                           rkeys + ['wdw', 'vecs'], [('ps', bkD)])
                    elif k < CW - 1:
                        STT(accv(bkD), srcv(k, fullb), wv[:, l, c, k:k + 1], accv(bkD), ALU.mult, ALU.add,
                            rkeys + ['wdw', ('ps', bkD)], [('ps', bkD)])
                    else:
                        STT(dstv, srcv(k, fullb), wv[:, l, c, k:k + 1], accv(bkD), ALU.mult, ALU.add,
                            rkeys + ['wdw', ('ps', bkD)], [('yc', c)])
                if NPE == CW:
                    TS(dstv, accv(bkP), V(('b_dw', l), c), None, ALU.add, None, [('ps', bkP), 'vecs'], [('yc', c)])
                elif NPE > 0:
                    TT(dstv, dstv, accv(bkP), ALU.add, [('yc', c), ('ps', bkP)], [('yc', c)])
                def stats(c):
                    q1 = nxt('sq', 4)
                    ACT(sq[q1][:, 0:w], yc[:, c, 0:w], AF.Copy, [('yc', c)], [('sq', q1)])
                    MM(ps[6][:, 0:w], ones_r[:, :], sq[q1][:, 0:w], c == 0, c == KC - 1, [('sq', q1), 'ones_r'], [('ps', 6)])
                    q2 = nxt('sq', 4)
                    ACT(sq[q2][:, 0:w], yc[:, c, 0:w], AF.Square, [('yc', c)], [('sq', q2)])
                    MM(ps[7][:, 0:w], ones_r[:, :], sq[q2][:, 0:w], c == 0, c == KC - 1, [('sq', q2), 'ones_r'], [('ps', 7)])
                pending.append(c)
                if len(pending) > 1:
                    stats(pending.pop(0))
            while pending:
                stats(pending.pop(0))
            ta = nxt('t32', 4)
            tb = nxt('t32', 4)
            ACT(t32[ta][:, 0:w], ps[6][:, 0:w], AF.Copy, [('ps', 6)], [('t32', ta)])
            TT(t32[tb][:, 0:w], t32[ta][:, 0:w], ps[6][:, 0:w], ALU.mult, [('t32', ta), ('ps', 6)], [('t32', tb)])
            TT(t32[tb][:, 0:w], ps[7][:, 0:w], t32[tb][:, 0:w], ALU.subtract, [('t32', tb), ('ps', 7)], [('t32', tb)])
            TS(t32[tb][:, 0:w], t32[tb][:, 0:w], 0.0, None, ALU.max, None, [('t32', tb)], [('t32', tb)])
            ACT(t32[tb][:, 0:w], t32[tb][:, 0:w], AF.Sqrt, [('t32', tb), 'epst'], [('t32', tb)], bias=epst[:, 0:1])
            RECIP(t32[tb][:, 0:w], t32[tb][:, 0:w], [('t32', tb)], [('t32', tb)])
            for c in range(KC):
                TT(yc[:, c, 0:w], yc[:, c, 0:w], ps[6][:, 0:w], ALU.subtract, [('yc', c), ('ps', 6)], [('yc', c)])
                TT(yc[:, c, 0:w], yc[:, c, 0:w], t32[tb][:, 0:w], ALU.mult, [('yc', c), ('t32', tb)], [('yc', c)])
                ACT(xn[:, c, off:off + w], yc[:, c, 0:w], AF.Silu, [('yc', c), 'vecs'], [('xn', c, ti)],
                    bias=V(('ln_b', l), c), scale=V(('ln_g', l), c))
        for half in range(2):
            si, sv = load_cols(w_pw2_d[l], [(half * 512, 512)])
            for ti, (off, w) in enumerate(tiles):
                for jj in range(4):
                    j = half * 4 + jj
                    bk = nxt('mm', 4)
                    for c in range(KC):
                        MM(ps[bk][:, 0:w], sv[:, c, jj * 128:(jj + 1) * 128], xn[:, c, off:off + w], c == 0, c == KC - 1,
                           [('wsl', si), ('xn', c, ti)], [('ps', bk)])
                    STT(h[:, j, off:off + w], ps[bk][:, 0:w], V(('b_pw2', l), j), h[:, j, off:off + w], ALU.add, ALU.add,
                        [('ps', bk), ('h', j, ti), 'vecs'], [('h', j, ti)])

    def ffn_layer(l, grp):
        tiles = grp['tiles']
        rmsnorm(tiles, ('norm_ffn', l), xn, 'xn')
        for p in range(FC // 2):
            si, sv = load_cols(w_in_d[l], [(p * 256, 256), (FF + p * 256, 256)])
            for ti, (off, w) in enumerate(tiles):
                for jj in range(2):
                    f = 2 * p + jj
                    ba = nxt('mm', 4)
                    bb = nxt('mm', 4)
                    for c in range(KC):
                        MM(ps[ba][:, 0:w], sv[:, c, jj * 128:(jj + 1) * 128], xn[:, c, off:off + w], c == 0, c == KC - 1,
                           [('wsl', si), ('xn', c, ti)], [('ps', ba)])
                    for c in range(KC):
                        MM(ps[bb][:, 0:w], sv[:, c, 256 + jj * 128:256 + (jj + 1) * 128], xn[:, c, off:off + w], c == 0, c == KC - 1,
                           [('wsl', si), ('xn', c, ti)], [('ps', bb)])
                    t = nxt('t32', 4)
                    ACT(t32[t][:, 0:w], ps[ba][:, 0:w], AF.Silu, [('ps', ba)], [('t32', t)])
                    TT(gbuf[:, f, off:off + w], t32[t][:, 0:w], ps[bb][:, 0:w], ALU.mult, [('t32', t), ('ps', bb), SCR],
                       [('g', f, ti), SCR])
        for j in range(KC):
            si, sv = load_wout(w_out_d[l], j)
            for ti, (off, w) in enumerate(tiles):
                bk = nxt('mm', 4)
                for f in range(FC):
                    MM(ps[bk][:, 0:w], sv[:, f, :], gbuf[:, f, off:off + w], f == 0, f == FC - 1,
                       [('wsl', si), ('g', f, ti)], [('ps', bk)])
                TT(h[:, j, off:off + w], ps[bk][:, 0:w], h[:, j, off:off + w], ALU.add, [('ps', bk), ('h', j, ti)], [('h', j, ti)])

    def ring_runs(blk_a, nblk):
        runs = []
        b = 0
        while b < nblk:
            slot = (blk_a + b) % RING
            n = min(nblk - b, RING - slot)
            runs.append((b, slot, n))
            b += n
        return runs

    def kv_stage(grp):
        tiles = grp['tiles']
        stiles = grp.get('stiles', ())
        ptiles = [ti_ for ti_ in range(len(tiles)) if ti_ not in stiles]
        rmsnorm(tiles, 'norm_kv', xn, 'xn')
        for half in range(2):
            si, sv = load_cols(w_kv_d, [(half * 512, 512)])
            for ti, (off, w) in enumerate(tiles):
                if ti in stiles:
                    continue
                for jj in range(4):
                    j = half * 4 + jj
                    bk = nxt('mm', 4)
                    for c in range(KC):
                        MM(ps[bk][:, 0:w], sv[:, c, jj * 128:(jj + 1) * 128], xn[:, c, off:off + w], c == 0, c == KC - 1,
                           [('wsl', si), ('xn', c, ti)], [('ps', bk)])
                    for (lb, slot, n) in ring_runs(grp['blk0'] + off // 128, w // 128):
                        ACT(ktr[:, j, slot * 128:(slot + n) * 128], ps[bk][:, lb * 128:(lb + n) * 128], AF.Copy,
                            [('ps', bk)], [('ktr', j, s2) for s2 in range(slot, slot + n)])
            if ptiles:
                outblks = [b for b in range(grp['ntok'] // 128) if grp['kind'] == 'main' and grp['last'] and b >= grp['ntok'] // 128 - 4]
                for b in outblks:
                    bk = nxt('mm', 4)
                    tix = [i for i, (o_, w_) in enumerate(tiles) if o_ <= b * 128 < o_ + w_][0]
                    for c in range(KC):
                        MM(ps[bk][:, :], xn[:, c, b * 128:(b + 1) * 128], sv[:, c, :], c == 0, c == KC - 1,
                           [('wsl', si), ('xn', c, tix)], [('ps', bk)])
                    t = nxt('t32', 4)
                    ACT(t32[t][:, :], ps[bk][:, :], AF.Copy, [('ps', bk)], [('t32', t)], tag='k')
                    ob = b - (grp['ntok'] // 128 - 4)
                    dma('sp', klast_d[half, ob * 128:(ob + 1) * 128, :], t32[t][:, :], [('t32', t)], [], chan=f"t32_{t}")
            for sti in stiles:
                soff = tiles[sti][0]
                for s_ in range(2):
                    bk = nxt('mm', 4)
                    for c in range(KC):
                        MM(ps[bk][0:16, :], xn[:, c, soff + s_ * 16:soff + (s_ + 1) * 16], sv[:, c, :], c == 0, c == KC - 1,
                           [('wsl', si), ('xn', c, sti)], [('ps', bk)])
                    t = nxt('t32', 4)
                    ACT(t32[t][0:16, :], ps[bk][0:16, :], AF.Copy, [('ps', bk)], [('t32', t)])
                    dma('sp', ksam_d[half, s_ * 16:(s_ + 1) * 16, :], t32[t][0:16, :], [('t32', t)], [('ksam', half, s_)], chan=f"t32_{t}")
        for half in range(2):
            si, sv = load_cols(w_kv_d, [(D + half * 512, 512)])
            if ptiles:
                nb = grp['ntok'] // 128
                for b in range(nb):
                    tix = [i for i, (o_, w_) in enumerate(tiles) if o_ <= b * 128 < o_ + w_][0]
                    bk = nxt('mm', 4)
                    for c in range(KC):
                        MM(ps[bk][:, :], xn[:, c, b * 128:(b + 1) * 128], sv[:, c, :], c == 0, c == KC - 1,
                           [('wsl', si), ('xn', c, tix)], [('ps', bk)])
                    slot = (grp['blk0'] + b) % RING
                    if not (grp['kind'] == 'main' and grp['last'] and b >= nb - 4):
                        CP(vr[:, slot, half * 512:(half + 1) * 512], ps[bk][:, :], [('ps', bk)], [('vr', slot, half)])
                    else:
                        t = nxt('t32', 4)
                        ACT(t32[t][:, :], ps[bk][:, :], AF.Copy, [('ps', bk)], [('t32', t)], tag='v')
                        CP(vr[:, slot, half * 512:(half + 1) * 512], t32[t][:, :], [('t32', t)], [('vr', slot, half)])
                        ob = b - (nb - 4)
                        dma('sp', vlast_d[half, ob * 128:(ob + 1) * 128, :], t32[t][:, :], [('t32', t)], [], chan=f"t32_{t}")
            for sti in stiles:
                soff = tiles[sti][0]
                for s_ in range(2):
                    bk = nxt('mm', 4)
                    for c in range(KC):
                        MM(ps[bk][0:16, :], xn[:, c, soff + s_ * 16:soff + (s_ + 1) * 16], sv[:, c, :], c == 0, c == KC - 1,
                           [('wsl', si), ('xn', c, sti)], [('ps', bk)])
                    t = nxt('t32', 4)
                    ACT(t32[t][0:16, :], ps[bk][0:16, :], AF.Copy, [('ps', bk)], [('t32', t)])
                    dma('sp', vsam_d[half, s_ * 16:(s_ + 1) * 16, :], t32[t][0:16, :], [('t32', t)], [('vsam', half, s_)], chan=f"t32_{t}")

    def attn_layer(jl, grp):
        tiles = grp['tiles']
        samp = grp['kind'] == 'samp'
        rmsnorm(tiles, ('norm_attn', jl), xn, 'xn')
        for half in range(2):
            si, sv = load_cols(w_q_d[jl], [(half * 512, 512)])
            for ti, (off, w) in enumerate(tiles):
                for jj in range(4):
                    j = half * 4 + jj
                    bk = nxt('mm', 4)
                    for c in range(KC):
                        MM(ps[bk][:, 0:w], sv[:, c, jj * 128:(jj + 1) * 128], xn[:, c, off:off + w], c == 0, c == KC - 1,
                           [('wsl', si), ('xn', c, ti)], [('ps', bk)])
                    ACT(QT[:, j, off:off + w], ps[bk][:, 0:w], AF.Copy, [('ps', bk), SCR], [('qt', j, ti), SCR], scale=HD ** -0.5)
        for hq in range(0, NH, 4):
            dma('sp', biasT[:, hq:hq + 4, :], btab_d[jl, hq:hq + 4, :, :].rearrange("h q k -> q h k"),
                [('qt', KC - 1, len(tiles) - 1)], ['biasT'], chan='bias', multi=(hq > 0))
        MEMSET(biasT[0:64, :, 192:256], NEG, ['biasT'])
        qgroups = []
        if not samp:
            for b in range(grp['ntok'] // 128):
                gb = grp['blk0'] + b
                qgroups.append(dict(q0=b * 128, nq=128, kblocks=[((gb - 4 + i) % RING, 128, grp['first'] and (gb - 4 + i) <= 4) for i in range(5)]))
        else:
            for s_ in range(2):
                qgroups.append(dict(q0=16 * s_, nq=16, kblocks=[(5 * s_ + i, 128 if i < 4 else 16, False) for i in range(5)]))
        items = []
        for qg in qgroups:
            q0, nq = qg['q0'], qg['nq']
            tix = [i for i, (o_, w_) in enumerate(tiles) if o_ <= q0 < o_ + w_][0]
            ncols = sum(kw for (_, kw, _) in qg['kblocks'])
            oi = nxt('otok', 2)
            for hd_ in range(NH):
                items.append(dict(q0=q0, nq=nq, tix=tix, ncols=ncols, oi=oi, hd=hd_, kb=qg['kblocks'], n=len(items)))
        NST = 8

        def stA(it):
            n, nq, hd_, q0 = it['n'], it['nq'], it['hd'], it['q0']
            cj, pb = hd_ // 2, (hd_ % 2) * 64
            sA, sB = (0, 1) if n % 2 == 0 else (2, 3)
            qT = QT[pb:pb + 64, cj, q0:q0 + nq]
            kb = it['kb']
            merged = (not any(hm for (_, _, hm) in kb[0:4])) and all(kb[i][0] == kb[0][0] + i and kb[i][1] == 128 for i in range(4))
            for i, (slot, kw, hm) in enumerate(kb):
                if merged and i < 4:
                    if i == 0:
                        MM(ps[sA][0:nq, 0:512], qT, ktr[pb:pb + 64, cj, slot * 128:slot * 128 + 512], True, True,
                           [('qt', cj, it['tix'])] + [('ktr', cj, slot + i2) for i2 in range(4)], [('ps', sA)])
                    continue
                outp = ps[sA][0:nq, i * 128:i * 128 + kw] if i < 4 else ps[sB][0:nq, 0:kw]
                okey = ('ps', sA) if i < 4 else ('ps', sB)
                MM(outp, qT, ktr[pb:pb + 64, cj, slot * 128:slot * 128 + kw], True, not hm,
                   [('qt', cj, it['tix']), ('ktr', cj, slot)], [okey])
                if hm:
                    MM(outp, ones_b[0:1, 0:nq], maskrow[0:1, 0:kw], False, True, ['ones_b', 'maskrow'], [okey])

        def stB(it, git=None):
            n, nq, hd_, ncols = it['n'], it['nq'], it['hd'], it['ncols']
            sA, sB = (0, 1) if n % 2 == 0 else (2, 3)
            sbi, st = n % 2, n % NST
            sbv = Sb[sbi]
            mx = small[:, 8 * st:8 * st + 3]
            TS(sbv[0:nq, 0:64], ps[sA][0:nq, 0:64], cA[0:nq, jl, hd_:hd_ + 1], None, ALU.add, ALU.max,
               [('ps', sA), ('cA', jl)], [('sb', sbi), ('mx', st)], accum_out=mx[0:nq, 0:1])
            TS(sbv[0:nq, 64:384], ps[sA][0:nq, 64:384], vecs[0:nq, VCOL[('cB', jl)] + hd_:VCOL[('cB', jl)] + hd_ + 1], None,
               ALU.add, ALU.max, [('ps', sA), 'vecs'], [('sb', sbi), ('mx', st)], accum_out=mx[0:nq, 1:2])
            TT(sbv[0:nq, 384:512], ps[sA][0:nq, 384:512], biasT[0:nq, hd_, 0:128], ALU.add, [('ps', sA), 'biasT'], [('sb3', sbi)])
            nb2 = ncols - 512
            TT(sbv[0:nq, 512:ncols], ps[sB][0:nq, 0:nb2], biasT[0:nq, hd_, 128:128 + nb2], ALU.add, [('ps', sB), 'biasT'], [('sb4', sbi)])
            if git is not None:
                gst = git['n'] % NST
                gnq = git['nq']
                RECIP(small[0:gnq, 8 * gst + 5:8 * gst + 6], small[0:gnq, 8 * gst + 4:8 * gst + 5], [('rs', gst)], [('ri', gst)])
            S.op('dve', lambda hh, a=mx[0:nq, 2:3], b_=sbv[0:nq, 384:ncols]: hh.reduce_max(a, b_, AX.X),
                 [('sb3', sbi), ('sb4', sbi)], [('mx2', st)])
            ng = small[:, 8 * st + 3:8 * st + 4]
            S.op('dve', lambda hh, a=ng[0:nq, :], b_=mx[0:nq, :]: hh.tensor_reduce(a, b_, AX.X, ALU.max, negate=True),
                 [('mx', st), ('mx2', st)], [('ng', st)])

        def stC(it):
            n, nq, ncols = it['n'], it['nq'], it['ncols']
            sbi, st = n % 2, n % NST
            ng = small[:, 8 * st + 3:8 * st + 4]
            rs = small[:, 8 * st + 4:8 * st + 5]
            ACT(Pb[sbi][0:nq, 0:ncols], Sb[sbi][0:nq, 0:ncols], AF.Exp, [('sb', sbi), ('sb3', sbi), ('sb4', sbi), ('ng', st)], [('pb', sbi), ('rs', st)],
                bias=ng[0:nq, :], accum_out=rs[0:nq, :])

        def stD(it):
            n, nq = it['n'], it['nq']
            sbi = n % 2
            tb_ = 4 + sbi
            ptv = ps[tb_][:, :].bitcast(BF16)
            for i, (slot, kw, hm) in enumerate(it['kb']):
                TR(ptv[0:kw, i * 128:i * 128 + nq], Pb[sbi][0:nq, i * 128:i * 128 + kw], ident_b[0:nq, 0:nq],
                   [('pb', sbi), 'ident_b'], [('ps', tb_)])

        def stE(it):
            n, nq = it['n'], it['nq']
            tb_ = 4 + n % 2
            p3 = n % 3
            ptv = ps[tb_][:, :].bitcast(BF16)
            ACT(PT[p3][:, :, 0:nq], ptv[:, 0:640].rearrange("p (b n) -> p b n", b=5)[:, :, 0:nq], AF.Copy,
                [('ps', tb_)], [('pt', p3)])

        def stF(it):
            n, nq, hd_ = it['n'], it['nq'], it['hd']
            p3 = n % 3
            ob_ = 6 + (hd_ % 2)
            oc = (hd_ // 2) * 64
            for i, (slot, kw, hm) in enumerate(it['kb']):
                MM(ps[ob_][0:nq, oc:oc + 64], PT[p3][0:kw, i, 0:nq], vr[0:kw, slot, hd_ * 64:(hd_ + 1) * 64], i == 0, i == 4,
                   [('pt', p3), ('vr', slot, hd_ // 8)], [('ps', ob_)])

        def stG(it):
            n, nq, hd_, oi, q0 = it['n'], it['nq'], it['hd'], it['oi'], it['q0']
            st = n % NST
            ob_ = 6 + (hd_ % 2)
            oc = (hd_ // 2) * 64
            rs = small[:, 8 * st + 4:8 * st + 5]
            ri = small[:, 8 * st + 5:8 * st + 6]
            ACT(Otok[oi][0:nq, hd_ * 64:(hd_ + 1) * 64], ps[ob_][0:nq, oc:oc + 64], AF.Copy, [('ps', ob_), ('ri', st)],
                [('otok', oi, hd_)], scale=ri[0:nq, :])
            if hd_ == NH - 1:
                tb_ = 4 + (n + 1) % 2
                otv = ps[tb_][:, :].bitcast(BF16)
                for c in range(KC):
                    TR(otv[:, c * 128:c * 128 + nq], Otok[oi][0:nq, c * 128:(c + 1) * 128], ident_b[0:nq, 0:nq],
                       [('otok', oi, 2 * c), ('otok', oi, 2 * c + 1), 'ident_b'], [('ps', tb_)])
                CP(xn[:, :, q0:q0 + nq], otv[:, :].rearrange("p (c n) -> p c n", c=8)[:, :, 0:nq], [('ps', tb_)],
                   [('xn', c, it['tix']) for c in range(KC)])

        NI = len(items)
        for s_ in range(NI + 4):
            if s_ < NI:
                stA(items[s_])
            if 0 <= s_ - 2 < NI:
                stD(items[s_ - 2])
                stE(items[s_ - 2])
            if 0 <= s_ - 3 < NI:
                stF(items[s_ - 3])
            if 0 <= s_ - 4 < NI:
                git = items[s_ - 4]
                gst = git['n'] % NST
                RECIP(small[0:git['nq'], 8 * gst + 5:8 * gst + 6], small[0:git['nq'], 8 * gst + 4:8 * gst + 5], [('rs', gst)], [('ri', gst)])
                stG(git)
            if s_ < NI:
                stB(items[s_], None)
                stC(items[s_])
        for half in range(2):
            si, sv = load_cols(w_o_d[jl], [(half * 512, 512)])
            for ti, (off, w) in enumerate(tiles):
                for jj in range(4):
                    j = half * 4 + jj
                    bk = nxt('mm', 4)
                    for c in range(KC):
                        MM(ps[bk][:, 0:w], sv[:, c, jj * 128:(jj + 1) * 128], xn[:, c, off:off + w], c == 0, c == KC - 1,
                           [('wsl', si), ('xn', c, ti)], [('ps', bk)])
                    TT(h[:, j, off:off + w], ps[bk][:, 0:w], h[:, j, off:off + w], ALU.add, [('ps', bk), ('h', j, ti)], [('h', j, ti)])

    def final_out(grp, out_ap):
        tiles = grp['tiles']
        rmsnorm(tiles, 'norm_final', ynf, 'ynf')
        for ti, (off, w) in enumerate(tiles):
            nb = max(1, w // 128)
            for b in range(nb):
                bw = min(128, w)
                si = nxt('stg', 2)
                for half in range(2):
                    bk = nxt('mm', 4)
                    for cc in range(4):
                        c = half * 4 + cc
                        TR(ps[bk][0:bw, cc * 128:(cc + 1) * 128], ynf[:, c, off + b * 128:off + b * 128 + bw], ident_f[:, :],
                           [('ynf', c, ti), 'ident_f'], [('ps', bk)])
                    if half == 0:
                        ACT(stg[si][0:bw, 0:512], ps[bk][0:bw, :], AF.Copy, [('ps', bk)], [('stg', si)])
                    else:
                        CP(stg[si][0:bw, 512:1024], ps[bk][0:bw, :], [('ps', bk)], [('stg', si)])
                dma('sp', out_ap[off + b * 128:off + b * 128 + bw, :], stg[si][0:bw, :], [('stg', si)], [], chan=f"stg{si}")

    def conv_out():
        for l in range(2):
            si = nxt('stg', 2)
            for half in range(2):
                bk = nxt('mm', 4)
                for cc in range(4):
                    c = half * 4 + cc
                    TR(ps[bk][0:CH, cc * 128:(cc + 1) * 128], hist[:, l, c, :], ident_f[:, :], ['hist', ('histw', c), 'ident_f'], [('ps', bk)])
                CP(stg[si][0:CH, half * 512:(half + 1) * 512], ps[bk][0:CH, :], [('ps', bk)], [('stg', si)])
            dma('sp', convp_d[l, :, :], stg[si][0:CH, :], [('stg', si)], [], chan=f"stg{si}")

    def load_cache():
        for s_ in range(2):
            for b in range(4):
                dma('pool', vr[:, 5 * s_ + b, :], cv_d[s_, b * 128:(b + 1) * 128, :], (),
                    [('vr', 5 * s_ + b, 0), ('vr', 5 * s_ + b, 1)], chan='cv', multi=True)
                si = nxt('stg', 2)
                dma('sp', stg[si][:, :], ck_d[s_, b * 128:(b + 1) * 128, :], (), [('stg', si)], chan=f"stg{si}")
                for half in range(2):
                    bk = nxt('mm', 4)
                    for cc in range(4):
                        c = half * 4 + cc
                        TR(ps[bk][:, cc * 128:(cc + 1) * 128], stg[si][:, c * 128:(c + 1) * 128], ident_f[:, :],
                           [('stg', si), 'ident_f'], [('ps', bk)])
                    o3 = ktr[:, half * 4:half * 4 + 4, (5 * s_ + b) * 128:(5 * s_ + b + 1) * 128]
                    i3 = ps[bk][:, :].rearrange("p (c n) -> p c n", c=4)
                    CP(o3, i3, [('ps', bk)], [('ktr', half * 4 + cc, 5 * s_ + b) for cc in range(4)])
            slot = 5 * s_ + 4
            for half in range(2):
                dma('pool', vr[0:16, slot, half * 512:(half + 1) * 512], vsam_d[half, s_ * 16:(s_ + 1) * 16, :],
                    [('vsam', half, s_)], [('vr', slot, half)], chan='cv', multi=True)
            si = nxt('stg', 2)
            for half in range(2):
                dma('sp', stg[si][0:16, half * 512:(half + 1) * 512], ksam_d[half, s_ * 16:(s_ + 1) * 16, :],
                    [('ksam', half, s_)], [('stg', si)], chan=f"stg{si}", multi=(half > 0))
            for half in range(2):
                bk = nxt('mm', 4)
                for cc in range(4):
                    c = half * 4 + cc
                    TR(ps[bk][:, cc * 128:cc * 128 + 16], stg[si][0:16, c * 128:(c + 1) * 128], ident_f[0:16, 0:16],
                       [('stg', si), 'ident_f'], [('ps', bk)])
                o3 = ktr[:, half * 4:half * 4 + 4, slot * 128:slot * 128 + 16]
                i3 = ps[bk][:, :].rearrange("p (c n) -> p c n", c=4)[:, :, 0:16]
                CP(o3, i3, [('ps', bk)], [('ktr', half * 4 + cc, slot) for cc in range(4)])

    groups = [
        dict(kind='halo', tiles=[(0, 256), (256, 384), (640, 32)], stiles=(2,), ntok=640, blk0=0, x0=0, last=False),
        dict(kind='main', tiles=[(0, 512), (512, 512)], ntok=1024, blk0=5, x0=640, last=False, first=True),
        dict(kind='main', tiles=[(0, 512), (512, 512)], ntok=1024, blk0=5, x0=1664, last=True, first=False),
        dict(kind='samp', tiles=[(0, 32)], ntok=32, blk0=None, x0=0, last=False),
    ]
    for gi, grp in enumerate(groups):
        if group_sel is not None and gi not in group_sel:
            continue
        if grp['kind'] == 'samp':
            conv_out()
            load_cache()
            CP(h[:, :, 0:32], h_s[:, :, :], ['h_s'], [('h', c, 0) for c in range(KC)])
        elif grp['kind'] == 'halo':
            load_x(x_d[0:640, :], grp['tiles'][0:2])
            load_x(xs_d, grp['tiles'][2:3], src_base=640, ti0=2)
        else:
            if grp['kind'] == 'main' and not grp['first']:
                ACT(ktr[:, :, 128:640], ktr[:, :, 9 * 128:13 * 128], AF.Copy,
                    [('ktr', j, s_) for j in range(KC) for s_ in range(9, 13)],
                    [('ktr', j, s_) for j in range(KC) for s_ in range(1, 5)])
                CP(vr[:, 1:5, :], vr[:, 9:13, :], [('vr', s_, hf) for s_ in range(9, 13) for hf in range(2)],
                   [('vr', s_, hf) for s_ in range(1, 5) for hf in range(2)])
            load_x(x_d[grp['x0']:grp['x0'] + grp['ntok'], :], grp['tiles'])
        if grp['kind'] != 'samp':
            for l in range(2):
                conv_layer(l, grp)
                ffn_layer(l, grp)
            kv_stage(grp)
        if grp['kind'] == 'halo':
            CP(h_s[:, :, :], h[:, :, 640:672], [('h', c, 2) for c in range(KC)], ['h_s'])
            continue
        for jl in range(2):
            attn_layer(jl, grp)
            ffn_layer(2 + jl, grp)
        if grp['kind'] == 'samp':
            final_out(grp, ys_d)
        else:
            final_out(grp, y_d[(gi - 1) * 1024:gi * 1024, :])

    S.finalize()
    sems = {e: es.enter_context(nc.semaphore(f"sem_{e}")) for e in ENGS}
    csems = {name: es.enter_context(nc.semaphore(f"c_{name}")) for name in S.chan_cnt}
    with nc.Block() as block:
        S.emit(nc, block, sems, csems)
    es.close()
    return nc


_CACHE = {}


def _layout_vec(v):
    return np.ascontiguousarray(np.asarray(v, np.float32).reshape(-1, 128).T)


def kernel(x_prompt, x_sample, cache_conv, cache_k, cache_v,
           norm_conv, w_pw1, b_pw1, w_dw, b_dw, ln_g, ln_b, w_pw2, b_pw2,
           norm_kv, w_kv, norm_attn, w_q, w_o, rel_bias,
           norm_ffn, w_ffn_in, w_ffn_out, norm_final):
    f = lambda a: np.ascontiguousarray(np.asarray(a, dtype=np.float32))
    x_prompt, x_sample, cache_conv, cache_k, cache_v = map(f, (x_prompt, x_sample, cache_conv, cache_k, cache_v))
    if 'nc' not in _CACHE:
        _CACHE['nc'] = build_program()
    nc = _CACHE['nc']
    vecs = np.zeros((128, NV), np.float32)

    def put(name, v):
        a = _layout_vec(v)
        vecs[:, VCOL[name]:VCOL[name] + a.shape[1]] = a
    rel_bias = f(rel_bias)
    for l in range(2):
        put(('norm_conv', l), norm_conv[l])
        put(('b_pw1a', l), np.asarray(b_pw1)[l, :D])
        put(('b_pw1g', l), np.asarray(b_pw1)[l, D:])
        put(('b_dw', l), b_dw[l])
        put(('ln_g', l), ln_g[l])
        put(('ln_b', l), ln_b[l])
        put(('b_pw2', l), b_pw2[l])
        put(('norm_attn', l), norm_attn[l])
        vecs[:, VCOL[('cB', l)]:VCOL[('cB', l)] + NH] = np.broadcast_to(rel_bias[l, :, 256][None, :], (128, NH))
    for l in range(4):
        put(('norm_ffn', l), norm_ffn[l])
    put('norm_kv', norm_kv)
    put('norm_final', norm_final)
    wdw = np.ascontiguousarray(f(w_dw).reshape(2, CW, KC, 128).transpose(3, 0, 2, 1)).reshape(128, 2 * KC * CW)
    idx = np.clip(np.arange(128)[:, None] - np.arange(256)[None, :] + 256, 0, 256)
    btab = np.ascontiguousarray(rel_bias[:, :, idx])
    shared = dict(w_pw1=f(w_pw1), w_pw2=f(w_pw2), w_kv=f(w_kv), w_q=f(w_q), w_o=f(w_o), w_ffn_in=f(w_ffn_in),
                  w_ffn_out=f(w_ffn_out), vecs=vecs, wdw=wdw, btab=btab,
                  ident=np.eye(128, dtype=np.float32))
    in_maps = []
    for c in range(NCORE):
        b, half = c // 2, c % 2
        xs = np.zeros((NX, D), np.float32)
        if half == 0:
            xs[HALO:] = x_prompt[b, 0:NTOK]
        else:
            xs[:] = x_prompt[b, NTOK - HALO:2 * NTOK]
        m = dict(shared)
        m['x'] = xs
        m['xs'] = np.ascontiguousarray(x_sample[2 * c:2 * c + 2].reshape(32, D))
        m['cconv'] = np.ascontiguousarray(cache_conv[:, 2 * c:2 * c + 2])
        m['ck'] = np.ascontiguousarray(cache_k[2 * c:2 * c + 2].reshape(2, 512, D))
        m['cv'] = np.ascontiguousarray(cache_v[2 * c:2 * c + 2].reshape(2, 512, D))
        m['flag'] = np.full((128, 1), float(half), np.float32)
        in_maps.append(m)
    res = run_bass_kernel_spmd(nc, in_maps, core_ids=list(range(NCORE)))
    R = res.results
    B = 4
    y_prompt = np.zeros((B, 2 * NTOK, D), np.float32)
    y_sample = np.zeros((16, 16, D), np.float32)
    conv_prompt = np.zeros((2, B, CH, D), np.float32)
    conv_sample = np.zeros((2, 16, CH, D), np.float32)
    k_prompt = np.zeros((B, 512, NH, HD), np.float32)
    v_prompt = np.zeros((B, 512, NH, HD), np.float32)
    k_sample = np.zeros((16, 16, NH, HD), np.float32)
    v_sample = np.zeros((16, 16, NH, HD), np.float32)
    for c in range(NCORE):
        b, half = c // 2, c % 2
        r = R[c]
        y_prompt[b, half * NTOK:(half + 1) * NTOK] = r['y']
        y_sample[2 * c:2 * c + 2] = r['ys'].reshape(2, 16, D)
        conv_sample[:, 2 * c:2 * c + 2] = r['convs']
        k_sample[2 * c:2 * c + 2] = r['ksam'].transpose(1, 0, 2).reshape(2, 16, NH, HD)
        v_sample[2 * c:2 * c + 2] = r['vsam'].transpose(1, 0, 2).reshape(2, 16, NH, HD)
        if half == 1:
            conv_prompt[:, b] = r['convp']
            k_prompt[b] = r['klast'].transpose(1, 0, 2).reshape(512, NH, HD)
            v_prompt[b] = r['vlast'].transpose(1, 0, 2).reshape(512, NH, HD)
    return (y_prompt, y_sample, conv_prompt, conv_sample, k_prompt, v_prompt, k_sample, v_sample)
```

```python
import numpy as np
import concourse.bass as bass
import concourse.mybir as mybir
from concourse.bass_utils import run_bass_kernel_spmd

F32 = mybir.dt.float32
F32R = mybir.dt.float32r
BF16 = mybir.dt.bfloat16
AF = mybir.ActivationFunctionType
ALU = mybir.AluOpType
AX = mybir.AxisListType

D = 1024
KC = 8
FF = 2816
FC = 22
NH = 16
HD = 64
CW = 31
CH = 30
NEG = -1.0e30
EPS = 1e-6
NCORE = 8
NTOK = 2048
HALO = 640
NX = NTOK + HALO
RING = 13
ENGS = ['pe', 'act', 'dve', 'pool', 'sp']
BLK = {'pe': 'tensor', 'act': 'scalar', 'dve': 'vector', 'pool': 'gpsimd', 'sp': 'sync'}


class Sched:
    def __init__(s):
        s.ops = {e: [] for e in ENGS}
        s.lastw = {}
        s.rds = {}
        s.chan_cnt = {}

    def op(s, eng, fn, r=(), w=(), chan=None, multi=False):
        idx = len(s.ops[eng])
        deps = {}

        def add(ref):
            if ref is None:
                return
            k, v = ref
            if deps.get(k, -1) < v:
                deps[k] = v
        r = [k_ for k_ in r if k_ != ('scr',)]
        w = [k_ for k_ in w if k_ != ('scr',)]
        for key in r:
            add(s.lastw.get(key))
        for key in w:
            add(s.lastw.get(key))
            for ref in s.rds.get(key, ()):
                add(ref)
        if chan is not None:
            c = s.chan_cnt.get(chan, 0) + 1
            s.chan_cnt[chan] = c
            ref = (('c', chan), c)
            if multi:
                deps.pop(('c', chan), None)
        else:
            ref = (('e', eng), idx)
        if eng == 'pe':
            deps.pop(('e', 'pe'), None)
        o = dict(fn=fn, deps=deps, sig=False, chan=chan, val=None)
        s.ops[eng].append(o)
        for key in w:
            s.lastw[key] = ref
            s.rds[key] = []
        for key in r:
            s.rds.setdefault(key, []).append(ref)
        return o

    def finalize(s):
        for e in ENGS:
            for o in s.ops[e]:
                for (k, name), v in o['deps'].items():
                    if k == 'e':
                        s.ops[name][v]['sig'] = True
        for e in ENGS:
            c = 0
            for o in s.ops[e]:
                if o['chan'] is None and o['sig']:
                    c += 1
                    o['val'] = c

    def emit(s, nc, block, sems, csems):
        for e in ENGS:
            if not s.ops[e]:
                continue

            def body(h, e=e):
                waited = {}
                for o in s.ops[e]:
                    for (k, name), v in o['deps'].items():
                        if k == 'e':
                            sem = sems[name]
                            val = s.ops[name][v]['val']
                        else:
                            sem = csems[name]
                            val = 16 * (s.chan_cnt[name] if name in ('init', 'cv') else v)
                        if waited.get((k, name), 0) >= val:
                            continue
                        waited[(k, name)] = val
                        h.wait_ge(sem, val)
                    ins = o['fn'](h)
                    if o['chan'] is not None:
                        ins.then_inc(csems[o['chan']], 16)
                    elif o['sig']:
                        ins.then_inc(sems[e], 1)
                if e == 'sp':
                    for name, cnt in s.chan_cnt.items():
                        h.wait_ge(csems[name], 16 * cnt)
            getattr(block, BLK[e])(body)


def vec_cols():
    cols = {}
    n = 0

    def add(name, k):
        nonlocal n
        cols[name] = n
        n += k
    for l in range(2):
        add(('norm_conv', l), 8)
        add(('b_pw1a', l), 8)
        add(('b_pw1g', l), 8)
        add(('b_dw', l), 8)
        add(('ln_g', l), 8)
        add(('ln_b', l), 8)
        add(('b_pw2', l), 8)
        add(('norm_attn', l), 8)
        add(('cB', l), 16)
    for l in range(4):
        add(('norm_ffn', l), 8)
    add('norm_kv', 8)
    add('norm_final', 8)
    return cols, n


VCOL, NV = vec_cols()


def build_program(group_sel=None):
    nc = bass.Bass("TRN2", target_bir_lowering=False)
    S = Sched()

    def din(name, shape):
        return nc.dram_tensor(name, shape, F32, kind="ExternalInput")

    def dout(name, shape):
        return nc.dram_tensor(name, shape, F32, kind="ExternalOutput")

    x_d = din("x", [NX, D]).ap()
    xs_d = din("xs", [32, D]).ap()
    cconv_d = din("cconv", [2, 2, CH, D]).ap()
    ck_d = din("ck", [2, 512, D]).ap()
    cv_d = din("cv", [2, 512, D]).ap()
    w_pw1_d = din("w_pw1", [2, D, 2 * D]).ap()
    w_pw2_d = din("w_pw2", [2, D, D]).ap()
    w_kv_d = din("w_kv", [D, 2 * D]).ap()
    w_q_d = din("w_q", [2, D, D]).ap()
    w_o_d = din("w_o", [2, D, D]).ap()
    w_in_d = din("w_ffn_in", [4, D, 2 * FF]).ap()
    w_out_d = din("w_ffn_out", [4, FF, D]).ap()
    vecs_d = din("vecs", [128, NV]).ap()
    wdw_d = din("wdw", [128, 2 * KC * CW]).ap()
    btab_d = din("btab", [2, NH, 128, 256]).ap()
    flag_d = din("flag", [128, 1]).ap()
    ident_d = din("ident", [128, 128]).ap()

    y_d = dout("y", [NTOK, D]).ap()
    ys_d = dout("ys", [32, D]).ap()
    convp_d = dout("convp", [2, CH, D]).ap()
    convs_d = dout("convs", [2, 2, CH, D]).ap()
    klast_d = dout("klast", [2, 512, 512]).ap()
    vlast_d = dout("vlast", [2, 512, 512]).ap()
    ksam_d = dout("ksam", [2, 32, 512]).ap()
    vsam_d = dout("vsam", [2, 32, 512]).ap()

    from contextlib import ExitStack
    es = ExitStack()

    def sb(name, shape, dt):
        return es.enter_context(nc.sbuf_tensor("sb_" + name, shape, dt))

    GM = 1024
    h = sb("h", [128, KC, GM], F32)
    xn = sb("xn", [128, KC, GM], BF16)
    NSCR = 8 * (GM + CH) + 8 * 512
    scr = sb("scr", [128, NSCR], F32)
    ktr = sb("ktr", [128, KC, RING * 128], BF16)
    vr = sb("vr", [128, RING, D], BF16)
    NSLOT = 3
    wsl = [sb(f"wsl{i}", [128, 4096], BF16) for i in range(NSLOT)]
    vecs = sb("vecs", [128, NV], F32)
    wdw = sb("wdw", [128, 2 * KC * CW], F32)
    flag = sb("flag", [128, 1], F32)
    hist = sb("hist", [128, 2, KC, CH], F32)
    ones_r = sb("ones_r", [128, 128], F32R)
    ones_f = sb("ones_f", [128, 128], F32)
    ident_f = sb("ident_f", [128, 128], F32)
    ident_b = sb("ident_b", [128, 128], BF16)
    ones_b = sb("ones_b", [1, 128], BF16)
    maskrow = sb("maskrow", [1, 128], BF16)
    epst = sb("epst", [128, 1], F32)
    cA = sb("cA", [128, 2, NH], F32)
    rowmask = sb("rowmask", [128, 1], F32)
    sq = [sb(f"sq{i}", [128, 512], F32R) for i in range(4)]
    t32 = [sb(f"t32_{i}", [128, 512], F32) for i in range(4)]
    stg = [sb(f"stg{i}", [128, D], F32) for i in range(2)]
    small = sb("small", [128, 64], F32)
    h_s = sb("h_s", [128, KC, 32], F32)
    NDG = 8
    dg = [sb(f"dg{i}", [128, 128], BF16) for i in range(NDG)]
    assert 11072 + 1024 <= NSCR
    ps = [es.enter_context(nc.psum_tensor(f"ps{i}", [128, 512], F32)) for i in range(8)]

    full = scr[:, 0:8 * (GM + CH)].rearrange("p (c n) -> p c n", c=8)
    FBW = GM + 32
    fullb = scr[:, 0:4 * FBW].bitcast(BF16).rearrange("p (c n) -> p c n", c=8)
    fulls = scr[:, 4 * FBW:4 * FBW + 8 * 92].rearrange("p (c n) -> p c n", c=8)
    yc = scr[:, 8 * (GM + CH):8 * (GM + CH) + 4096].rearrange("p (c n) -> p c n", c=8)
    gbuf = scr[:, 0:11264].bitcast(BF16).rearrange("p (f n) -> p f n", f=FC)
    QT = scr[:, 0:4096].bitcast(BF16).rearrange("p (c n) -> p c n", c=8)
    biasT = scr[:, 4096:8192].rearrange("p (h n) -> p h n", h=NH)
    Sb = [scr[:, 8192 + i * 640:8192 + (i + 1) * 640] for i in range(2)]
    Pb = [scr[:, 9472 + i * 320:9472 + (i + 1) * 320].bitcast(BF16) for i in range(2)]
    PT = [scr[:, 10112 + i * 320:10112 + (i + 1) * 320].bitcast(BF16).rearrange("p (b n) -> p b n", b=5) for i in range(3)]
    Otok = [scr[:, 11072 + i * 512:11072 + (i + 1) * 512].bitcast(BF16) for i in range(2)]
    ynf = scr[:, 0:8192].rearrange("p (c n) -> p c n", c=8)

    scr_state = {'stage': None, 'fence': []}

    def scr_keys(stage):
        return ('scrstage',)

    V = lambda name, c=None: vecs[:, VCOL[name] + (0 if c is None else c):VCOL[name] + (0 if c is None else c) + 1]

    def dma(q, out, in_, r, w, chan, multi=False, **kw):
        S.op(q, lambda hh: hh.dma_start(out=out, in_=in_, **kw), r, w, chan=chan, multi=multi)

    def MM(out, lhsT, rhs, start, stop, r, w):
        S.op('pe', lambda hh: hh.matmul(out, lhsT, rhs, start=start, stop=stop), r, w)

    def TR(out, in_, ident, r, w):
        S.op('pe', lambda hh: hh.transpose(out, in_, ident), r, w)

    def ACT(out, in_, func, r, w, bias=None, scale=None, accum_out=None, tag=None):
        kw = {}
        if bias is not None:
            kw['bias'] = bias
        if scale is not None:
            kw['scale'] = scale
        if accum_out is not None:
            kw['accum_out'] = accum_out
        S.op('act', lambda hh: hh.activation(out, in_, func, **kw), r, w)

    def TT(out, in0, in1, op, r, w, eng='dve'):
        S.op(eng, lambda hh: hh.tensor_tensor(out, in0, in1, op), r, w)

    def TS(out, in0, s1, s2, op0, op1, r, w, accum_out=None, eng='dve'):
        if op1 is None:
            S.op(eng, lambda hh: hh.tensor_scalar(out, in0, s1, s2, op0, accum_out=accum_out), r, w)
        else:
            S.op(eng, lambda hh: hh.tensor_scalar(out, in0, s1, s2, op0, op1, accum_out=accum_out), r, w)

    def STT(out, in0, sc, in1, op0, op1, r, w):
        S.op('dve', lambda hh: hh.scalar_tensor_tensor(out, in0, sc, in1, op0, op1), r, w)

    def CP(out, in_, r, w, eng='dve'):
        S.op(eng, lambda hh: hh.tensor_copy(out, in_), r, w)

    def MEMSET(ap, val, w, eng='dve'):
        S.op(eng, lambda hh: hh.memset(ap, val), (), w)

    def RECIP(out, in_, r, w):
        S.op('dve', lambda hh: hh.reciprocal(out, in_), r, w)

    SCR = ('scr',)
    rr = {'mm': 0, 'w': 0, 'sq': 0, 't32': 0, 'stg': 0, 'otok': 0, 'dg': 0}

    def nxt(name, n):
        v = rr[name]
        rr[name] = (v + 1) % n
        return v

    def load_cols(wd, col_lists):
        si = nxt('w', NSLOT)
        sv = wsl[si][:, :].rearrange("p (c n) -> p c n", c=8)
        src = wd.rearrange("(c p) n -> p c n", p=128)
        o = 0
        for (c0, ncol) in col_lists:
            dma('pool', sv[:, :, o:o + ncol], src[:, :, c0:c0 + ncol], (), [('wsl', si)], chan=f"wsl{si}", multi=(o > 0))
            o += ncol
        return si, sv

    def load_wout(wd, j):
        si = nxt('w', NSLOT)
        sv = wsl[si][:, 0:FC * 128].rearrange("p (f n) -> p f n", f=FC)
        src = wd.rearrange("(f p) n -> p f n", p=128)
        dma('pool', sv, src[:, :, j * 128:(j + 1) * 128], (), [('wsl', si)], chan=f"wsl{si}")
        return si, sv

    dma('sp', vecs[:, :], vecs_d, (), ['vecs'], chan='init')
    dma('sp', wdw[:, :], wdw_d, (), ['wdw'], chan='init')
    dma('sp', flag[:, :], flag_d, (), ['flag'], chan='init')
    MEMSET(ones_f[:, :], 1.0 / D, ['ones_f'])
    ACT(ones_r[:, :], ones_f[:, :], AF.Copy, ['ones_f'], ['ones_r'])
    MEMSET(ones_b[:, :], 1.0, ['ones_b'])
    MEMSET(epst[:, :], EPS, ['epst'])
    MEMSET(hist[:, :, :, :], 0.0, ['hist'])
    dma('sp', ident_f[:, :], ident_d, (), ['ident_f'], chan='init')
    CP(ident_b[:, :], ident_f[:, :], ['ident_f'], ['ident_b'])
    MEMSET(rowmask[:, :], 0.0, ['rowmask'])
    MEMSET(rowmask[64:128, :], NEG, ['rowmask'])
    TS(maskrow[0:1, :], ones_b[0:1, :], flag[0:1, 0:1], -1.0, ALU.mult, ALU.add, ['ones_b', 'flag'], ['maskrow'])
    TS(maskrow[0:1, :], maskrow[0:1, :], -NEG, None, ALU.mult, None, ['maskrow'], ['maskrow'])
    for l in range(2):
        TS(cA[:, l, :], vecs[:, VCOL[('cB', l)]:VCOL[('cB', l)] + NH], rowmask[:, 0:1], None, ALU.add, None,
           ['vecs', 'rowmask'], [('cA', l)])

    def load_x(src_ap, tiles, src_base=0, ti0=0):
        for ti_, (off, w) in enumerate(tiles):
            ti = ti0 + ti_
            nb = max(1, w // 128)
            for b in range(nb):
                bw = min(128, w)
                si = nxt('stg', 2)
                dma('sp', stg[si][0:bw, :], src_ap[off - src_base + b * 128:off - src_base + b * 128 + bw, :], (), [('stg', si)], chan=f"stg{si}")
                for half in range(2):
                    bk = nxt('mm', 4)
                    for cc in range(4):
                        c = half * 4 + cc
                        TR(ps[bk][:, cc * 128:cc * 128 + bw], stg[si][0:bw, c * 128:(c + 1) * 128], ident_f[0:bw, 0:bw],
                           [('stg', si), 'ident_f'], [('ps', bk)])
                    o3 = h[:, half * 4:half * 4 + 4, off + b * 128:off + b * 128 + bw]
                    i3 = ps[bk][:, :].rearrange("p (c n) -> p c n", c=4)[:, :, 0:bw]
                    wk = [('h', half * 4 + cc, ti) for cc in range(4)]
                    if half == 0:
                        ACT(o3, i3, AF.Copy, [('ps', bk)], wk)
                    else:
                        CP(o3, i3, [('ps', bk)], wk)

    def rmsnorm(tiles, gname, dst, dkey):
        for ti, (off, w) in enumerate(tiles):
            for c in range(KC):
                qi = nxt('sq', 4)
                ACT(sq[qi][:, 0:w], h[:, c, off:off + w], AF.Square, [('h', c, ti)], [('sq', qi)])
                MM(ps[7][:, 0:w], ones_r[:, :], sq[qi][:, 0:w], c == 0, c == KC - 1, [('sq', qi), 'ones_r'], [('ps', 7)])
            ti2 = nxt('t32', 4)
            ACT(t32[ti2][:, 0:w], ps[7][:, 0:w], AF.Sqrt, [('ps', 7), 'epst'], [('t32', ti2)], bias=epst[:, 0:1])
            RECIP(t32[ti2][:, 0:w], t32[ti2][:, 0:w], [('t32', ti2)], [('t32', ti2)])
            for c in range(KC):
                STT(dst[:, c, off:off + w], h[:, c, off:off + w], V(gname, c), t32[ti2][:, 0:w], ALU.mult, ALU.mult,
                    [('h', c, ti), ('t32', ti2), 'vecs'], [(dkey, c, ti), SCR] if dkey == 'ynf' else [(dkey, c, ti)])

    def conv_layer(l, grp):
        tiles = grp['tiles']
        ntok = grp['ntok']
        stiles = grp.get('stiles', ())
        ptiles = [ti_ for ti_ in range(len(tiles)) if ti_ not in stiles]
        last_p = ptiles[-1] if ptiles else None
        SB0 = 704
        rmsnorm(tiles, ('norm_conv', l), xn, 'xn')
        if ptiles:
            CP(fullb[:, :, 0:CH], hist[:, l, :, :], ['hist'] + [('histw', j) for j in range(KC)], [('fullh',)])
        if stiles:
            for s_ in range(2):
                si = nxt('stg', 2)
                dma('sp', stg[si][0:CH, :], cconv_d[l, s_, :, :], (), [('stg', si)], chan=f"stg{si}")
                for half in range(2):
                    bk = nxt('mm', 4)
                    for cc in range(4):
                        c = half * 4 + cc
                        TR(ps[bk][:, cc * 128:cc * 128 + CH], stg[si][0:CH, c * 128:(c + 1) * 128], ident_f[0:CH, 0:CH],
                           [('stg', si), 'ident_f'], [('ps', bk)])
                    o3 = fulls[:, half * 4:half * 4 + 4, 46 * s_:46 * s_ + CH]
                    i3 = ps[bk][:, :].rearrange("p (c n) -> p c n", c=4)[:, :, 0:CH]
                    CP(o3, i3, [('ps', bk), SCR], [('fullsh',), SCR])
        for p in range(4):
            si, sv = load_cols(w_pw1_d[l], [(p * 256, 256), (D + p * 256, 256)])
            for ti, (off, w) in enumerate(tiles):
                for jj in range(2):
                    j = 2 * p + jj
                    ba = nxt('mm', 4)
                    bg = nxt('mm', 4)
                    for c in range(KC):
                        MM(ps[ba][:, 0:w], sv[:, c, jj * 128:(jj + 1) * 128], xn[:, c, off:off + w], c == 0, c == KC - 1,
                           [('wsl', si), ('xn', c, ti)], [('ps', ba)])
                    for c in range(KC):
                        MM(ps[bg][:, 0:w], sv[:, c, 256 + jj * 128:256 + (jj + 1) * 128], xn[:, c, off:off + w], c == 0, c == KC - 1,
                           [('wsl', si), ('xn', c, ti)], [('ps', bg)])
                    t = nxt('t32', 4)
                    ACT(t32[t][:, 0:w], ps[bg][:, 0:w], AF.Sigmoid, [('ps', bg), 'vecs'], [('t32', t)], bias=V(('b_pw1g', l), j))
                    if ti not in stiles:
                        dstv = fullb[:, j, CH + off:CH + off + w]
                        STT(dstv, ps[ba][:, 0:w], V(('b_pw1a', l), j), t32[t][:, 0:w], ALU.add, ALU.mult,
                            [('ps', ba), ('t32', t), 'vecs', SCR], [('full', j, ti), SCR])
                        if ti == last_p:
                            STT(hist[:, l, j, :], ps[ba][:, w - CH:w], V(('b_pw1a', l), j), t32[t][:, w - CH:w], ALU.add, ALU.mult,
                                [('ps', ba), ('t32', t), 'vecs', ('fullh',)], [('histw', j)])
                        if grp['kind'] == 'halo':
                            TS(dstv, dstv, flag[:, 0:1], None, ALU.mult, None, [('full', j, ti), 'flag'], [('full', j, ti)])
                            if ti == last_p:
                                TS(hist[:, l, j, :], hist[:, l, j, :], flag[:, 0:1], None, ALU.mult, None, [('histw', j), 'flag'], [('histw', j)])
                    else:
                        dstv = fulls[:, j, 0:92].rearrange("p (s t) -> p s t", s=2)[:, :, CH:CH + 16]
                        STT(dstv, ps[ba][:, 0:32].rearrange("p (s t) -> p s t", s=2), V(('b_pw1a', l), j),
                            t32[t][:, 0:32].rearrange("p (s t) -> p s t", s=2), ALU.add, ALU.mult,
                            [('ps', ba), ('t32', t), 'vecs', SCR], [('full', j, ti), SCR])
        allfull = [('full', j, ti_) for j in range(KC) for ti_ in stiles] + [('fullsh',)]
        if stiles:
            CP(fullb[:, :, SB0:SB0 + 92], fulls[:, :, :], allfull, [('fullsb',)])
            for s_ in range(2):
                si = nxt('stg', 2)
                for half in range(2):
                    bk = nxt('mm', 4)
                    for cc in range(4):
                        c = half * 4 + cc
                        TR(ps[bk][0:CH, cc * 128:(cc + 1) * 128], fulls[:, c, 46 * s_ + 16:46 * s_ + 46], ident_f[:, :],
                           allfull + ['ident_f'], [('ps', bk)])
                    CP(stg[si][0:CH, half * 512:(half + 1) * 512], ps[bk][0:CH, :], [('ps', bk)], [('stg', si)])
                dma('sp', convs_d[l, s_, :, :], stg[si][0:CH, :], [('stg', si)], [], chan=f"stg{si}")
        wv = wdw[:, :].rearrange("p (l c k) -> p l c k", l=2, c=KC)
        for ti, (off, w) in enumerate(tiles):
            rk = [('fullh',)] + [('full', None, t2) for t2 in range(ti + 1)]
            pending = []
            NPE = 26
            samp = ti in stiles
            for c in range(KC):
                bkD = 4 + (c % 2)
                bkP = 2 + (c % 2)
                if samp:
                    rkeys = [('fullsb',)]
                else:
                    rkeys = [('fullh',), ('full', c, ti)] + ([('full', c, ti - 1)] if (ti > 0 and (ti - 1) not in stiles) else [])

                def srcv(k, fv):
                    if not samp:
                        return fv[:, c, off + k:off + k + w]
                    return fv[:, c, SB0:SB0 + 92].rearrange("p (s t) -> p s t", s=2)[:, :, k:k + 16]

                def accv(bk):
                    if not samp:
                        return ps[bk][:, 0:w]
                    return ps[bk][:, 0:32].rearrange("p (s t) -> p s t", s=2)
                dstv = yc[:, c, 0:w] if not samp else yc[:, c, 0:32].rearrange("p (s t) -> p s t", s=2)
                for k in range(NPE):
                    di = nxt('dg', NDG)
                    if k % 2 == 0:
                        S.op('pool', lambda hh, o_=dg[di][:, :], s1=wv[:, l, c, k:k + 1]:
                             hh.tensor_scalar(o_, ident_f[:, :], s1, 0.0, ALU.mult, ALU.add), ['ident_f', 'wdw'], [('dg', di)])
                    else:
                        ACT(dg[di][:, :], ident_f[:, :], AF.Copy, ['ident_f', 'wdw'], [('dg', di)], scale=wv[:, l, c, k:k + 1])
                    MM(accv(bkP), dg[di][:, :], srcv(k, fullb), k == 0, k == NPE - 1, rkeys + [('dg', di)], [('ps', bkP)])
                for k in range(NPE, CW):
                    if k == NPE:
                        TS(accv(bkD), srcv(k, fullb), wv[:, l, c, k:k + 1], V(('b_dw', l), c), ALU.mult, ALU.add,
                           rkeys + ['wdw', 'vecs'], [('ps', bkD)])
                    elif k < CW - 1:
                        STT(accv(bkD), srcv(k, fullb), wv[:, l, c, k:k + 1], accv(bkD), ALU.mult, ALU.add,
                            rkeys + ['wdw', ('ps', bkD)], [('ps', bkD)])
                    else:
                        STT(dstv, srcv(k, fullb), wv[:, l, c, k:k + 1], accv(bkD), ALU.mult, ALU.add,
                            rkeys + ['wdw', ('ps', bkD)], [('yc', c)])
                if NPE == CW:
                    TS(dstv, accv(bkP), V(('b_dw', l), c), None, ALU.add, None, [('ps', bkP), 'vecs'], [('yc', c)])
                elif NPE > 0:
                    TT(dstv, dstv, accv(bkP), ALU.add, [('yc', c), ('ps', bkP)], [('yc', c)])
                def stats(c):
                    q1 = nxt('sq', 4)
                    ACT(sq[q1][:, 0:w], yc[:, c, 0:w], AF.Copy, [('yc', c)], [('sq', q1)])
                    MM(ps[6][:, 0:w], ones_r[:, :], sq[q1][:, 0:w], c == 0, c == KC - 1, [('sq', q1), 'ones_r'], [('ps', 6)])
                    q2 = nxt('sq', 4)
                    ACT(sq[q2][:, 0:w], yc[:, c, 0:w], AF.Square, [('yc', c)], [('sq', q2)])
                    MM(ps[7][:, 0:w], ones_r[:, :], sq[q2][:, 0:w], c == 0, c == KC - 1, [('sq', q2), 'ones_r'], [('ps', 7)])
                pending.append(c)
                if len(pending) > 1:
                    stats(pending.pop(0))
            while pending:
                stats(pending.pop(0))
            ta = nxt('t32', 4)
            tb = nxt('t32', 4)
            ACT(t32[ta][:, 0:w], ps[6][:, 0:w], AF.Copy, [('ps', 6)], [('t32', ta)])
            TT(t32[tb][:, 0:w], t32[ta][:, 0:w], ps[6][:, 0:w], ALU.mult, [('t32', ta), ('ps', 6)], [('t32', tb)])
            TT(t32[tb][:, 0:w], ps[7][:, 0:w], t32[tb][:, 0:w], ALU.subtract, [('t32', tb), ('ps', 7)], [('t32', tb)])
            TS(t32[tb][:, 0:w], t32[tb][:, 0:w], 0.0, None, ALU.max, None, [('t32', tb)], [('t32', tb)])
            ACT(t32[tb][:, 0:w], t32[tb][:, 0:w], AF.Sqrt, [('t32', tb), 'epst'], [('t32', tb)], bias=epst[:, 0:1])
            RECIP(t32[tb][:, 0:w], t32[tb][:, 0:w], [('t32', tb)], [('t32', tb)])
            for c in range(KC):
                TT(yc[:, c, 0:w], yc[:, c, 0:w], ps[6][:, 0:w], ALU.subtract, [('yc', c), ('ps', 6)], [('yc', c)])
                TT(yc[:, c, 0:w], yc[:, c, 0:w], t32[tb][:, 0:w], ALU.mult, [('yc', c), ('t32', tb)], [('yc', c)])
                ACT(xn[:, c, off:off + w], yc[:, c, 0:w], AF.Silu, [('yc', c), 'vecs'], [('xn', c, ti)],
                    bias=V(('ln_b', l), c), scale=V(('ln_g', l), c))
        for half in range(2):
            si, sv = load_cols(w_pw2_d[l], [(half * 512, 512)])
            for ti, (off, w) in enumerate(tiles):
                for jj in range(4):
                    j = half * 4 + jj
                    bk = nxt('mm', 4)
                    for c in range(KC):
                        MM(ps[bk][:, 0:w], sv[:, c, jj * 128:(jj + 1) * 128], xn[:, c, off:off + w], c == 0, c == KC - 1,
                           [('wsl', si), ('xn', c, ti)], [('ps', bk)])
                    STT(h[:, j, off:off + w], ps[bk][:, 0:w], V(('b_pw2', l), j), h[:, j, off:off + w], ALU.add, ALU.add,
                        [('ps', bk), ('h', j, ti), 'vecs'], [('h', j, ti)])

    def ffn_layer(l, grp):
        tiles = grp['tiles']
        rmsnorm(tiles, ('norm_ffn', l), xn, 'xn')
        for p in range(FC // 2):
            si, sv = load_cols(w_in_d[l], [(p * 256, 256), (FF + p * 256, 256)])
            for ti, (off, w) in enumerate(tiles):
                for jj in range(2):
                    f = 2 * p + jj
                    ba = nxt('mm', 4)
                    bb = nxt('mm', 4)
                    for c in range(KC):
                        MM(ps[ba][:, 0:w], sv[:, c, jj * 128:(jj + 1) * 128], xn[:, c, off:off + w], c == 0, c == KC - 1,
                           [('wsl', si), ('xn', c, ti)], [('ps', ba)])
                    for c in range(KC):
                        MM(ps[bb][:, 0:w], sv[:, c, 256 + jj * 128:256 + (jj + 1) * 128], xn[:, c, off:off + w], c == 0, c == KC - 1,
                           [('wsl', si), ('xn', c, ti)], [('ps', bb)])
                    t = nxt('t32', 4)
                    ACT(t32[t][:, 0:w], ps[ba][:, 0:w], AF.Silu, [('ps', ba)], [('t32', t)])
                    TT(gbuf[:, f, off:off + w], t32[t][:, 0:w], ps[bb][:, 0:w], ALU.mult, [('t32', t), ('ps', bb), SCR],
                       [('g', f, ti), SCR])
        for j in range(KC):
            si, sv = load_wout(w_out_d[l], j)
            for ti, (off, w) in enumerate(tiles):
                bk = nxt('mm', 4)
                for f in range(FC):
                    MM(ps[bk][:, 0:w], sv[:, f, :], gbuf[:, f, off:off + w], f == 0, f == FC - 1,
                       [('wsl', si), ('g', f, ti)], [('ps', bk)])
                TT(h[:, j, off:off + w], ps[bk][:, 0:w], h[:, j, off:off + w], ALU.add, [('ps', bk), ('h', j, ti)], [('h', j, ti)])

    def ring_runs(blk_a, nblk):
        runs = []
        b = 0
        while b < nblk:
            slot = (blk_a + b) % RING
            n = min(nblk - b, RING - slot)
            runs.append((b, slot, n))
            b += n
        return runs

    def kv_stage(grp):
        tiles = grp['tiles']
        stiles = grp.get('stiles', ())
        ptiles = [ti_ for ti_ in range(len(tiles)) if ti_ not in stiles]
        rmsnorm(tiles, 'norm_kv', xn, 'xn')
        for half in range(2):
            si, sv = load_cols(w_kv_d, [(half * 512, 512)])
            for ti, (off, w) in enumerate(tiles):
                if ti in stiles:
                    continue
                for jj in range(4):
                    j = half * 4 + jj
                    bk = nxt('mm', 4)
                    for c in range(KC):
                        MM(ps[bk][:, 0:w], sv[:, c, jj * 128:(jj + 1) * 128], xn[:, c, off:off + w], c == 0, c == KC - 1,
                           [('wsl', si), ('xn', c, ti)], [('ps', bk)])
                    for (lb, slot, n) in ring_runs(grp['blk0'] + off // 128, w // 128):
                        ACT(ktr[:, j, slot * 128:(slot + n) * 128], ps[bk][:, lb * 128:(lb + n) * 128], AF.Copy,
                            [('ps', bk)], [('ktr', j, s2) for s2 in range(slot, slot + n)])
            if ptiles:
                outblks = [b for b in range(grp['ntok'] // 128) if grp['kind'] == 'main' and grp['last'] and b >= grp['ntok'] // 128 - 4]
                for b in outblks:
                    bk = nxt('mm', 4)
                    tix = [i for i, (o_, w_) in enumerate(tiles) if o_ <= b * 128 < o_ + w_][0]
                    for c in range(KC):
                        MM(ps[bk][:, :], xn[:, c, b * 128:(b + 1) * 128], sv[:, c, :], c == 0, c == KC - 1,
                           [('wsl', si), ('xn', c, tix)], [('ps', bk)])
                    t = nxt('t32', 4)
                    ACT(t32[t][:, :], ps[bk][:, :], AF.Copy, [('ps', bk)], [('t32', t)], tag='k')
                    ob = b - (grp['ntok'] // 128 - 4)
                    dma('sp', klast_d[half, ob * 128:(ob + 1) * 128, :], t32[t][:, :], [('t32', t)], [], chan=f"t32_{t}")
            for sti in stiles:
                soff = tiles[sti][0]
                for s_ in range(2):
                    bk = nxt('mm', 4)
                    for c in range(KC):
                        MM(ps[bk][0:16, :], xn[:, c, soff + s_ * 16:soff + (s_ + 1) * 16], sv[:, c, :], c == 0, c == KC - 1,
                           [('wsl', si), ('xn', c, sti)], [('ps', bk)])
                    t = nxt('t32', 4)
                    ACT(t32[t][0:16, :], ps[bk][0:16, :], AF.Copy, [('ps', bk)], [('t32', t)])
                    dma('sp', ksam_d[half, s_ * 16:(s_ + 1) * 16, :], t32[t][0:16, :], [('t32', t)], [('ksam', half, s_)], chan=f"t32_{t}")
        for half in range(2):
            si, sv = load_cols(w_kv_d, [(D + half * 512, 512)])
            if ptiles:
                nb = grp['ntok'] // 128
                for b in range(nb):
                    tix = [i for i, (o_, w_) in enumerate(tiles) if o_ <= b * 128 < o_ + w_][0]
                    bk = nxt('mm', 4)
                    for c in range(KC):
                        MM(ps[bk][:, :], xn[:, c, b * 128:(b + 1) * 128], sv[:, c, :], c == 0, c == KC - 1,
                           [('wsl', si), ('xn', c, tix)], [('ps', bk)])
                    slot = (grp['blk0'] + b) % RING
                    if not (grp['kind'] == 'main' and grp['last'] and b >= nb - 4):
                        CP(vr[:, slot, half * 512:(half + 1) * 512], ps[bk][:, :], [('ps', bk)], [('vr', slot, half)])
                    else:
                        t = nxt('t32', 4)
                        ACT(t32[t][:, :], ps[bk][:, :], AF.Copy, [('ps', bk)], [('t32', t)], tag='v')
                        CP(vr[:, slot, half * 512:(half + 1) * 512], t32[t][:, :], [('t32', t)], [('vr', slot, half)])
                        ob = b - (nb - 4)
                        dma('sp', vlast_d[half, ob * 128:(ob + 1) * 128, :], t32[t][:, :], [('t32', t)], [], chan=f"t32_{t}")
            for sti in stiles:
                soff = tiles[sti][0]
                for s_ in range(2):
                    bk = nxt('mm', 4)
                    for c in range(KC):
                        MM(ps[bk][0:16, :], xn[:, c, soff + s_ * 16:soff + (s_ + 1) * 16], sv[:, c, :], c == 0, c == KC - 1,
                           [('wsl', si), ('xn', c, sti)], [('ps', bk)])
                    t = nxt('t32', 4)
                    ACT(t32[t][0:16, :], ps[bk][0:16, :], AF.Copy, [('ps', bk)], [('t32', t)])
                    dma('sp', vsam_d[half, s_ * 16:(s_ + 1) * 16, :], t32[t][0:16, :], [('t32', t)], [('vsam', half, s_)], chan=f"t32_{t}")

    def attn_layer(jl, grp):
        tiles = grp['tiles']
        samp = grp['kind'] == 'samp'
        rmsnorm(tiles, ('norm_attn', jl), xn, 'xn')
        for half in range(2):
            si, sv = load_cols(w_q_d[jl], [(half * 512, 512)])
            for ti, (off, w) in enumerate(tiles):
                for jj in range(4):
                    j = half * 4 + jj
                    bk = nxt('mm', 4)
                    for c in range(KC):
                        MM(ps[bk][:, 0:w], sv[:, c, jj * 128:(jj + 1) * 128], xn[:, c, off:off + w], c == 0, c == KC - 1,
                           [('wsl', si), ('xn', c, ti)], [('ps', bk)])
                    ACT(QT[:, j, off:off + w], ps[bk][:, 0:w], AF.Copy, [('ps', bk), SCR], [('qt', j, ti), SCR], scale=HD ** -0.5)
        for hq in range(0, NH, 4):
            dma('sp', biasT[:, hq:hq + 4, :], btab_d[jl, hq:hq + 4, :, :].rearrange("h q k -> q h k"),
                [('qt', KC - 1, len(tiles) - 1)], ['biasT'], chan='bias', multi=(hq > 0))
        MEMSET(biasT[0:64, :, 192:256], NEG, ['biasT'])
        qgroups = []
        if not samp:
            for b in range(grp['ntok'] // 128):
                gb = grp['blk0'] + b
                qgroups.append(dict(q0=b * 128, nq=128, kblocks=[((gb - 4 + i) % RING, 128, grp['first'] and (gb - 4 + i) <= 4) for i in range(5)]))
        else:
            for s_ in range(2):
                qgroups.append(dict(q0=16 * s_, nq=16, kblocks=[(5 * s_ + i, 128 if i < 4 else 16, False) for i in range(5)]))
        items = []
        for qg in qgroups:
            q0, nq = qg['q0'], qg['nq']
            tix = [i for i, (o_, w_) in enumerate(tiles) if o_ <= q0 < o_ + w_][0]
            ncols = sum(kw for (_, kw, _) in qg['kblocks'])
            oi = nxt('otok', 2)
            for hd_ in range(NH):
                items.append(dict(q0=q0, nq=nq, tix=tix, ncols=ncols, oi=oi, hd=hd_, kb=qg['kblocks'], n=len(items)))
        NST = 8

        def stA(it):
            n, nq, hd_, q0 = it['n'], it['nq'], it['hd'], it['q0']
            cj, pb = hd_ // 2, (hd_ % 2) * 64
            sA, sB = (0, 1) if n % 2 == 0 else (2, 3)
            qT = QT[pb:pb + 64, cj, q0:q0 + nq]
            kb = it['kb']
            merged = (not any(hm for (_, _, hm) in kb[0:4])) and all(kb[i][0] == kb[0][0] + i and kb[i][1] == 128 for i in range(4))
            for i, (slot, kw, hm) in enumerate(kb):
                if merged and i < 4:
                    if i == 0:
                        MM(ps[sA][0:nq, 0:512], qT, ktr[pb:pb + 64, cj, slot * 128:slot * 128 + 512], True, True,
                           [('qt', cj, it['tix'])] + [('ktr', cj, slot + i2) for i2 in range(4)], [('ps', sA)])
                    continue
                outp = ps[sA][0:nq, i * 128:i * 128 + kw] if i < 4 else ps[sB][0:nq, 0:kw]
                okey = ('ps', sA) if i < 4 else ('ps', sB)
                MM(outp, qT, ktr[pb:pb + 64, cj, slot * 128:slot * 128 + kw], True, not hm,
                   [('qt', cj, it['tix']), ('ktr', cj, slot)], [okey])
                if hm:
                    MM(outp, ones_b[0:1, 0:nq], maskrow[0:1, 0:kw], False, True, ['ones_b', 'maskrow'], [okey])

        def stB(it, git=None):
            n, nq, hd_, ncols = it['n'], it['nq'], it['hd'], it['ncols']
            sA, sB = (0, 1) if n % 2 == 0 else (2, 3)
            sbi, st = n % 2, n % NST
            sbv = Sb[sbi]
            mx = small[:, 8 * st:8 * st + 3]
            TS(sbv[0:nq, 0:64], ps[sA][0:nq, 0:64], cA[0:nq, jl, hd_:hd_ + 1], None, ALU.add, ALU.max,
               [('ps', sA), ('cA', jl)], [('sb', sbi), ('mx', st)], accum_out=mx[0:nq, 0:1])
            TS(sbv[0:nq, 64:384], ps[sA][0:nq, 64:384], vecs[0:nq, VCOL[('cB', jl)] + hd_:VCOL[('cB', jl)] + hd_ + 1], None,
               ALU.add, ALU.max, [('ps', sA), 'vecs'], [('sb', sbi), ('mx', st)], accum_out=mx[0:nq, 1:2])
            TT(sbv[0:nq, 384:512], ps[sA][0:nq, 384:512], biasT[0:nq, hd_, 0:128], ALU.add, [('ps', sA), 'biasT'], [('sb3', sbi)])
            nb2 = ncols - 512
            TT(sbv[0:nq, 512:ncols], ps[sB][0:nq, 0:nb2], biasT[0:nq, hd_, 128:128 + nb2], ALU.add, [('ps', sB), 'biasT'], [('sb4', sbi)])
            if git is not None:
                gst = git['n'] % NST
                gnq = git['nq']
                RECIP(small[0:gnq, 8 * gst + 5:8 * gst + 6], small[0:gnq, 8 * gst + 4:8 * gst + 5], [('rs', gst)], [('ri', gst)])
            S.op('dve', lambda hh, a=mx[0:nq, 2:3], b_=sbv[0:nq, 384:ncols]: hh.reduce_max(a, b_, AX.X),
                 [('sb3', sbi), ('sb4', sbi)], [('mx2', st)])
            ng = small[:, 8 * st + 3:8 * st + 4]
            S.op('dve', lambda hh, a=ng[0:nq, :], b_=mx[0:nq, :]: hh.tensor_reduce(a, b_, AX.X, ALU.max, negate=True),
                 [('mx', st), ('mx2', st)], [('ng', st)])

        def stC(it):
            n, nq, ncols = it['n'], it['nq'], it['ncols']
            sbi, st = n % 2, n % NST
            ng = small[:, 8 * st + 3:8 * st + 4]
            rs = small[:, 8 * st + 4:8 * st + 5]
            ACT(Pb[sbi][0:nq, 0:ncols], Sb[sbi][0:nq, 0:ncols], AF.Exp, [('sb', sbi), ('sb3', sbi), ('sb4', sbi), ('ng', st)], [('pb', sbi), ('rs', st)],
                bias=ng[0:nq, :], accum_out=rs[0:nq, :])

        def stD(it):
            n, nq = it['n'], it['nq']
            sbi = n % 2
            tb_ = 4 + sbi
            ptv = ps[tb_][:, :].bitcast(BF16)
            for i, (slot, kw, hm) in enumerate(it['kb']):
                TR(ptv[0:kw, i * 128:i * 128 + nq], Pb[sbi][0:nq, i * 128:i * 128 + kw], ident_b[0:nq, 0:nq],
                   [('pb', sbi), 'ident_b'], [('ps', tb_)])

        def stE(it):
            n, nq = it['n'], it['nq']
            tb_ = 4 + n % 2
            p3 = n % 3
            ptv = ps[tb_][:, :].bitcast(BF16)
            ACT(PT[p3][:, :, 0:nq], ptv[:, 0:640].rearrange("p (b n) -> p b n", b=5)[:, :, 0:nq], AF.Copy,
                [('ps', tb_)], [('pt', p3)])

        def stF(it):
            n, nq, hd_ = it['n'], it['nq'], it['hd']
            p3 = n % 3
            ob_ = 6 + (hd_ % 2)
            oc = (hd_ // 2) * 64
            for i, (slot, kw, hm) in enumerate(it['kb']):
                MM(ps[ob_][0:nq, oc:oc + 64], PT[p3][0:kw, i, 0:nq], vr[0:kw, slot, hd_ * 64:(hd_ + 1) * 64], i == 0, i == 4,
                   [('pt', p3), ('vr', slot, hd_ // 8)], [('ps', ob_)])

        def stG(it):
            n, nq, hd_, oi, q0 = it['n'], it['nq'], it['hd'], it['oi'], it['q0']
            st = n % NST
            ob_ = 6 + (hd_ % 2)
            oc = (hd_ // 2) * 64
            rs = small[:, 8 * st + 4:8 * st + 5]
            ri = small[:, 8 * st + 5:8 * st + 6]
            ACT(Otok[oi][0:nq, hd_ * 64:(hd_ + 1) * 64], ps[ob_][0:nq, oc:oc + 64], AF.Copy, [('ps', ob_), ('ri', st)],
                [('otok', oi, hd_)], scale=ri[0:nq, :])
            if hd_ == NH - 1:
                tb_ = 4 + (n + 1) % 2
                otv = ps[tb_][:, :].bitcast(BF16)
                for c in range(KC):
                    TR(otv[:, c * 128:c * 128 + nq], Otok[oi][0:nq, c * 128:(c + 1) * 128], ident_b[0:nq, 0:nq],
                       [('otok', oi, 2 * c), ('otok', oi, 2 * c + 1), 'ident_b'], [('ps', tb_)])
                CP(xn[:, :, q0:q0 + nq], otv[:, :].rearrange("p (c n) -> p c n", c=8)[:, :, 0:nq], [('ps', tb_)],
                   [('xn', c, it['tix']) for c in range(KC)])

        NI = len(items)
        for s_ in range(NI + 4):
            if s_ < NI:
                stA(items[s_])
            if 0 <= s_ - 2 < NI:
                stD(items[s_ - 2])
                stE(items[s_ - 2])
            if 0 <= s_ - 3 < NI:
                stF(items[s_ - 3])
            if 0 <= s_ - 4 < NI:
                git = items[s_ - 4]
                gst = git['n'] % NST
                RECIP(small[0:git['nq'], 8 * gst + 5:8 * gst + 6], small[0:git['nq'], 8 * gst + 4:8 * gst + 5], [('rs', gst)], [('ri', gst)])
                stG(git)
            if s_ < NI:
                stB(items[s_], None)
                stC(items[s_])
        for half in range(2):
            si, sv = load_cols(w_o_d[jl], [(half * 512, 512)])
            for ti, (off, w) in enumerate(tiles):
                for jj in range(4):
                    j = half * 4 + jj
                    bk = nxt('mm', 4)
                    for c in range(KC):
                        MM(ps[bk][:, 0:w], sv[:, c, jj * 128:(jj + 1) * 128], xn[:, c, off:off + w], c == 0, c == KC - 1,
                           [('wsl', si), ('xn', c, ti)], [('ps', bk)])
                    TT(h[:, j, off:off + w], ps[bk][:, 0:w], h[:, j, off:off + w], ALU.add, [('ps', bk), ('h', j, ti)], [('h', j, ti)])

    def final_out(grp, out_ap):
        tiles = grp['tiles']
        rmsnorm(tiles, 'norm_final', ynf, 'ynf')
        for ti, (off, w) in enumerate(tiles):
            nb = max(1, w // 128)
            for b in range(nb):
                bw = min(128, w)
                si = nxt('stg', 2)
                for half in range(2):
                    bk = nxt('mm', 4)
                    for cc in range(4):
                        c = half * 4 + cc
                        TR(ps[bk][0:bw, cc * 128:(cc + 1) * 128], ynf[:, c, off + b * 128:off + b * 128 + bw], ident_f[:, :],
                           [('ynf', c, ti), 'ident_f'], [('ps', bk)])
                    if half == 0:
                        ACT(stg[si][0:bw, 0:512], ps[bk][0:bw, :], AF.Copy, [('ps', bk)], [('stg', si)])
                    else:
                        CP(stg[si][0:bw, 512:1024], ps[bk][0:bw, :], [('ps', bk)], [('stg', si)])
                dma('sp', out_ap[off + b * 128:off + b * 128 + bw, :], stg[si][0:bw, :], [('stg', si)], [], chan=f"stg{si}")

    def conv_out():
        for l in range(2):
            si = nxt('stg', 2)
            for half in range(2):
                bk = nxt('mm', 4)
                for cc in range(4):
                    c = half * 4 + cc
                    TR(ps[bk][0:CH, cc * 128:(cc + 1) * 128], hist[:, l, c, :], ident_f[:, :], ['hist', ('histw', c), 'ident_f'], [('ps', bk)])
                CP(stg[si][0:CH, half * 512:(half + 1) * 512], ps[bk][0:CH, :], [('ps', bk)], [('stg', si)])
            dma('sp', convp_d[l, :, :], stg[si][0:CH, :], [('stg', si)], [], chan=f"stg{si}")

    def load_cache():
        for s_ in range(2):
            for b in range(4):
                dma('pool', vr[:, 5 * s_ + b, :], cv_d[s_, b * 128:(b + 1) * 128, :], (),
                    [('vr', 5 * s_ + b, 0), ('vr', 5 * s_ + b, 1)], chan='cv', multi=True)
                si = nxt('stg', 2)
                dma('sp', stg[si][:, :], ck_d[s_, b * 128:(b + 1) * 128, :], (), [('stg', si)], chan=f"stg{si}")
                for half in range(2):
                    bk = nxt('mm', 4)
                    for cc in range(4):
                        c = half * 4 + cc
                        TR(ps[bk][:, cc * 128:(cc + 1) * 128], stg[si][:, c * 128:(c + 1) * 128], ident_f[:, :],
                           [('stg', si), 'ident_f'], [('ps', bk)])
                    o3 = ktr[:, half * 4:half * 4 + 4, (5 * s_ + b) * 128:(5 * s_ + b + 1) * 128]
                    i3 = ps[bk][:, :].rearrange("p (c n) -> p c n", c=4)
                    CP(o3, i3, [('ps', bk)], [('ktr', half * 4 + cc, 5 * s_ + b) for cc in range(4)])
            slot = 5 * s_ + 4
            for half in range(2):
                dma('pool', vr[0:16, slot, half * 512:(half + 1) * 512], vsam_d[half, s_ * 16:(s_ + 1) * 16, :],
                    [('vsam', half, s_)], [('vr', slot, half)], chan='cv', multi=True)
            si = nxt('stg', 2)
            for half in range(2):
                dma('sp', stg[si][0:16, half * 512:(half + 1) * 512], ksam_d[half, s_ * 16:(s_ + 1) * 16, :],
                    [('ksam', half, s_)], [('stg', si)], chan=f"stg{si}", multi=(half > 0))
            for half in range(2):
                bk = nxt('mm', 4)
                for cc in range(4):
                    c = half * 4 + cc
                    TR(ps[bk][:, cc * 128:cc * 128 + 16], stg[si][0:16, c * 128:(c + 1) * 128], ident_f[0:16, 0:16],
                       [('stg', si), 'ident_f'], [('ps', bk)])
                o3 = ktr[:, half * 4:half * 4 + 4, slot * 128:slot * 128 + 16]
                i3 = ps[bk][:, :].rearrange("p (c n) -> p c n", c=4)[:, :, 0:16]
                CP(o3, i3, [('ps', bk)], [('ktr', half * 4 + cc, slot) for cc in range(4)])

    groups = [
        dict(kind='halo', tiles=[(0, 256), (256, 384), (640, 32)], stiles=(2,), ntok=640, blk0=0, x0=0, last=False),
        dict(kind='main', tiles=[(0, 512), (512, 512)], ntok=1024, blk0=5, x0=640, last=False, first=True),
        dict(kind='main', tiles=[(0, 512), (512, 512)], ntok=1024, blk0=5, x0=1664, last=True, first=False),
        dict(kind='samp', tiles=[(0, 32)], ntok=32, blk0=None, x0=0, last=False),
    ]
    for gi, grp in enumerate(groups):
        if group_sel is not None and gi not in group_sel:
            continue
        if grp['kind'] == 'samp':
            conv_out()
            load_cache()
            CP(h[:, :, 0:32], h_s[:, :, :], ['h_s'], [('h', c, 0) for c in range(KC)])
        elif grp['kind'] == 'halo':
            load_x(x_d[0:640, :], grp['tiles'][0:2])
            load_x(xs_d, grp['tiles'][2:3], src_base=640, ti0=2)
        else:
            if grp['kind'] == 'main' and not grp['first']:
                ACT(ktr[:, :, 128:640], ktr[:, :, 9 * 128:13 * 128], AF.Copy,
                    [('ktr', j, s_) for j in range(KC) for s_ in range(9, 13)],
                    [('ktr', j, s_) for j in range(KC) for s_ in range(1, 5)])
                CP(vr[:, 1:5, :], vr[:, 9:13, :], [('vr', s_, hf) for s_ in range(9, 13) for hf in range(2)],
                   [('vr', s_, hf) for s_ in range(1, 5) for hf in range(2)])
            load_x(x_d[grp['x0']:grp['x0'] + grp['ntok'], :], grp['tiles'])
        if grp['kind'] != 'samp':
            for l in range(2):
                conv_layer(l, grp)
                ffn_layer(l, grp)
            kv_stage(grp)
        if grp['kind'] == 'halo':
            CP(h_s[:, :, :], h[:, :, 640:672], [('h', c, 2) for c in range(KC)], ['h_s'])
            continue
        for jl in range(2):
            attn_layer(jl, grp)
            ffn_layer(2 + jl, grp)
        if grp['kind'] == 'samp':
            final_out(grp, ys_d)
        else:
            final_out(grp, y_d[(gi - 1) * 1024:gi * 1024, :])

    S.finalize()
    sems = {e: es.enter_context(nc.semaphore(f"sem_{e}")) for e in ENGS}
    csems = {name: es.enter_context(nc.semaphore(f"c_{name}")) for name in S.chan_cnt}
    with nc.Block() as block:
        S.emit(nc, block, sems, csems)
    es.close()
    return nc


_CACHE = {}


def _layout_vec(v):
    return np.ascontiguousarray(np.asarray(v, np.float32).reshape(-1, 128).T)


def kernel(x_prompt, x_sample, cache_conv, cache_k, cache_v,
           norm_conv, w_pw1, b_pw1, w_dw, b_dw, ln_g, ln_b, w_pw2, b_pw2,
           norm_kv, w_kv, norm_attn, w_q, w_o, rel_bias,
           norm_ffn, w_ffn_in, w_ffn_out, norm_final):
    f = lambda a: np.ascontiguousarray(np.asarray(a, dtype=np.float32))
    x_prompt, x_sample, cache_conv, cache_k, cache_v = map(f, (x_prompt, x_sample, cache_conv, cache_k, cache_v))
    if 'nc' not in _CACHE:
        _CACHE['nc'] = build_program()
    nc = _CACHE['nc']
    vecs = np.zeros((128, NV), np.float32)

    def put(name, v):
        a = _layout_vec(v)
        vecs[:, VCOL[name]:VCOL[name] + a.shape[1]] = a
    rel_bias = f(rel_bias)
    for l in range(2):
        put(('norm_conv', l), norm_conv[l])
        put(('b_pw1a', l), np.asarray(b_pw1)[l, :D])
        put(('b_pw1g', l), np.asarray(b_pw1)[l, D:])
        put(('b_dw', l), b_dw[l])
        put(('ln_g', l), ln_g[l])
        put(('ln_b', l), ln_b[l])
        put(('b_pw2', l), b_pw2[l])
        put(('norm_attn', l), norm_attn[l])
        vecs[:, VCOL[('cB', l)]:VCOL[('cB', l)] + NH] = np.broadcast_to(rel_bias[l, :, 256][None, :], (128, NH))
    for l in range(4):
        put(('norm_ffn', l), norm_ffn[l])
    put('norm_kv', norm_kv)
    put('norm_final', norm_final)
    wdw = np.ascontiguousarray(f(w_dw).reshape(2, CW, KC, 128).transpose(3, 0, 2, 1)).reshape(128, 2 * KC * CW)
    idx = np.clip(np.arange(128)[:, None] - np.arange(256)[None, :] + 256, 0, 256)
    btab = np.ascontiguousarray(rel_bias[:, :, idx])
    shared = dict(w_pw1=f(w_pw1), w_pw2=f(w_pw2), w_kv=f(w_kv), w_q=f(w_q), w_o=f(w_o), w_ffn_in=f(w_ffn_in),
                  w_ffn_out=f(w_ffn_out), vecs=vecs, wdw=wdw, btab=btab,
                  ident=np.eye(128, dtype=np.float32))
    in_maps = []
    for c in range(NCORE):
        b, half = c // 2, c % 2
        xs = np.zeros((NX, D), np.float32)
        if half == 0:
            xs[HALO:] = x_prompt[b, 0:NTOK]
        else:
            xs[:] = x_prompt[b, NTOK - HALO:2 * NTOK]
        m = dict(shared)
        m['x'] = xs
        m['xs'] = np.ascontiguousarray(x_sample[2 * c:2 * c + 2].reshape(32, D))
        m['cconv'] = np.ascontiguousarray(cache_conv[:, 2 * c:2 * c + 2])
        m['ck'] = np.ascontiguousarray(cache_k[2 * c:2 * c + 2].reshape(2, 512, D))
        m['cv'] = np.ascontiguousarray(cache_v[2 * c:2 * c + 2].reshape(2, 512, D))
        m['flag'] = np.full((128, 1), float(half), np.float32)
        in_maps.append(m)
    res = run_bass_kernel_spmd(nc, in_maps, core_ids=list(range(NCORE)))
    R = res.results
    B = 4
    y_prompt = np.zeros((B, 2 * NTOK, D), np.float32)
    y_sample = np.zeros((16, 16, D), np.float32)
    conv_prompt = np.zeros((2, B, CH, D), np.float32)
    conv_sample = np.zeros((2, 16, CH, D), np.float32)
    k_prompt = np.zeros((B, 512, NH, HD), np.float32)
    v_prompt = np.zeros((B, 512, NH, HD), np.float32)
    k_sample = np.zeros((16, 16, NH, HD), np.float32)
    v_sample = np.zeros((16, 16, NH, HD), np.float32)
    for c in range(NCORE):
        b, half = c // 2, c % 2
        r = R[c]
        y_prompt[b, half * NTOK:(half + 1) * NTOK] = r['y']
        y_sample[2 * c:2 * c + 2] = r['ys'].reshape(2, 16, D)
        conv_sample[:, 2 * c:2 * c + 2] = r['convs']
        k_sample[2 * c:2 * c + 2] = r['ksam'].transpose(1, 0, 2).reshape(2, 16, NH, HD)
        v_sample[2 * c:2 * c + 2] = r['vsam'].transpose(1, 0, 2).reshape(2, 16, NH, HD)
        if half == 1:
            conv_prompt[:, b] = r['convp']
            k_prompt[b] = r['klast'].transpose(1, 0, 2).reshape(512, NH, HD)
            v_prompt[b] = r['vlast'].transpose(1, 0, 2).reshape(512, NH, HD)
    return (y_prompt, y_sample, conv_prompt, conv_sample, k_prompt, v_prompt, k_sample, v_sample)
```
